# Optimizing a Trainium2 kernel written in Bass

```python
import math
import jax, jax.numpy as jnp
from jax import lax
import numpy as np

D_MODEL = 2048
BATCH = 4
SEQ = 4096
DEPTH = 4

HEAD_DIM = 128
N_BRANCHES = 4
BRANCH_WIDTH = D_MODEL // N_BRANCHES
CONV_CH = BRANCH_WIDTH
CONV_WIDTH = 31
SGU_CH = BRANCH_WIDTH
SGU_CHUNK = 128
SGU_GROUPS = SGU_CH // HEAD_DIM
DIL_HEADS = BRANCH_WIDTH // HEAD_DIM
DIL_PATTERNS = ((128, 1), (512, 4), (2048, 16))
DIL_BLOCK = 128
DIFF_HEADS = BRANCH_WIDTH // HEAD_DIM
DIFF_QK_DIM = HEAD_DIM // 2
DIFF_BLOCK = 128
ROPE_THETA = 500000.0
ROPE_FRACTION = 4
D_FF = 5632
EPS = 1e-6
NEG_INF = -1e30

A_COLS = 2 * CONV_CH
B_COLS = 2 * SGU_CH
C_COLS = 3 * DIL_HEADS * HEAD_DIM
DQK_COLS = DIFF_HEADS * 2 * DIFF_QK_DIM
DV_COLS = DIFF_HEADS * HEAD_DIM
GATE_COLS = N_BRANCHES * D_MODEL
COL_WIDTHS = (A_COLS, B_COLS, C_COLS, DQK_COLS, DQK_COLS, DV_COLS, GATE_COLS)
IN_COLS = sum(COL_WIDTHS)
SPLIT_POINTS = tuple(sum(COL_WIDTHS[: i + 1]) for i in range(len(COL_WIDTHS) - 1))

kernel_name = "macaron_gated_hybrid_conv_sgu_dilated_diffattn"


def rmsnorm(x, g):
    xf = x.astype(jnp.float32)
    y = xf * lax.rsqrt(jnp.mean(xf * xf, axis=-1, keepdims=True) + EPS)
    return (y * g.astype(jnp.float32)).astype(x.dtype)


def layernorm(x, g, b):
    xf = x.astype(jnp.float32)
    mu = jnp.mean(xf, axis=-1, keepdims=True)
    xc = xf - mu
    y = xc * lax.rsqrt(jnp.mean(xc * xc, axis=-1, keepdims=True) + EPS)
    return (y * g.astype(jnp.float32) + b.astype(jnp.float32)).astype(x.dtype)


def partial_rope(x, positions):
    dh = x.shape[-1]
    rot = dh // ROPE_FRACTION
    half = rot // 2
    inv_freq = 1.0 / (ROPE_THETA ** (jnp.arange(half, dtype=jnp.float32) * 2.0 / rot))
    ang = positions.astype(jnp.float32)[..., None] * inv_freq
    ang = ang.reshape(ang.shape[:2] + (1,) * (x.ndim - 3) + (half,))
    cos, sin = jnp.cos(ang), jnp.sin(ang)
    xf = x.astype(jnp.float32)
    x1, x2, xp = xf[..., :half], xf[..., half:rot], xf[..., rot:]
    out = jnp.concatenate([x1 * cos - x2 * sin, x2 * cos + x1 * sin, xp], axis=-1)
    return out.astype(x.dtype)


def swiglu_ffn(x, g, w13, w2):
    h = rmsnorm(x, g)
    gate, up = jnp.split(h @ w13, 2, axis=-1)
    return (jax.nn.silu(gate) * up) @ w2


def conformer_conv(a_in, conv_w, conv_b, ln_g, ln_b):
    a, gate = jnp.split(a_in, 2, axis=-1)
    z = a * jax.nn.sigmoid(gate)
    z = lax.conv_general_dilated(
        z, conv_w[:, None, :], window_strides=(1,),
        padding=((CONV_WIDTH - 1, 0),),
        dimension_numbers=("NWC", "WIO", "NWC"),
        feature_group_count=CONV_CH) + conv_b
    z = layernorm(z, ln_g, ln_b)
    return jax.nn.silu(z)


def spatial_gating(b_in, ln_g, ln_b, w_s, b_s):
    z = jax.nn.gelu(b_in, approximate=False)
    u, v = jnp.split(z, 2, axis=-1)
    v = layernorm(v, ln_g, ln_b)
    bn, s, _ = v.shape
    nc = s // SGU_CHUNK
    v = v.reshape(bn, nc, SGU_CHUNK, SGU_GROUPS, HEAD_DIM)
    causal = jnp.tril(jnp.ones((SGU_CHUNK, SGU_CHUNK), dtype=bool))
    w = jnp.where(causal[None], w_s, jnp.zeros((), w_s.dtype))
    mixed = jnp.einsum("gts,bcsgd->bctgd", w, v) + b_s.T[None, None, :, :, None]
    return u * mixed.reshape(bn, s, SGU_CH)


def _strided_window(q, k, v, span, dil):
    bn, s, h, dh = q.shape
    L = s // dil
    nb = -(-L // DIL_BLOCK)
    lp = nb * DIL_BLOCK

    def to_sub(t):
        return jnp.swapaxes(t.reshape(bn, L, dil, h, dh), 1, 2)

    qs, ks, vs = to_sub(q), to_sub(k), to_sub(v)
    qb = jnp.pad(qs, ((0, 0), (0, 0), (0, lp - L), (0, 0), (0, 0))).reshape(bn, dil, nb, DIL_BLOCK, h, dh)
    kp = jnp.pad(ks, ((0, 0), (0, 0), (DIL_BLOCK, lp - L), (0, 0), (0, 0))).reshape(bn, dil, nb + 1, DIL_BLOCK, h, dh)
    vp = jnp.pad(vs, ((0, 0), (0, 0), (DIL_BLOCK, lp - L), (0, 0), (0, 0))).reshape(bn, dil, nb + 1, DIL_BLOCK, h, dh)
    kband = jnp.concatenate([kp[:, :, :-1], kp[:, :, 1:]], axis=3)
    vband = jnp.concatenate([vp[:, :, :-1], vp[:, :, 1:]], axis=3)

    scores = jnp.einsum("brnqhd,brnkhd->brnqhk", qb, kband) * (dh ** -0.5)
    qi = jnp.arange(DIL_BLOCK)[:, None]
    kk = jnp.arange(2 * DIL_BLOCK)[None, :]
    dist = qi + DIL_BLOCK - kk
    in_band = (dist >= 0) & (dist <= span)
    key_sub = jnp.arange(nb)[:, None, None] * DIL_BLOCK - DIL_BLOCK + kk[None]
    mask = in_band[None] & (key_sub >= 0)
    scores = jnp.where(mask[None, None, :, :, None, :], scores, NEG_INF)
    m = jnp.max(scores, axis=-1)
    p = jnp.exp(scores - m[..., None])
    den = jnp.sum(p, axis=-1)
    o = jnp.einsum("brnqhk,brnkhd->brnqhd", p, vband) / den[..., None]

    def from_sub(t):
        t = t.reshape((bn, dil, lp) + t.shape[4:])[:, :, :L]
        t = jnp.swapaxes(t, 1, 2)
        return t.reshape((bn, s) + t.shape[3:])

    return from_sub(o), from_sub(m), from_sub(den)


def dilated_window_attention(q, k, v):
    dtype = v.dtype
    qf, kf, vf = q.astype(jnp.float32), k.astype(jnp.float32), v.astype(jnp.float32)
    res = [_strided_window(qf, kf, vf, w // d, d) for (w, d) in DIL_PATTERNS]
    m_all = jnp.max(jnp.stack([r[1] for r in res], axis=0), axis=0)
    wts = [r[2] * jnp.exp(r[1] - m_all) for r in res]
    num = sum(wg[..., None] * r[0] for wg, r in zip(wts, res))
    out = num / sum(wts)[..., None]
    bn, s, h, dh = q.shape
    return out.reshape(bn, s, h * dh).astype(dtype)


def diff_attention(q, k, v, lam, lam_init, subln_g):
    bn, s, h, _, dq = q.shape
    nb = s // DIFF_BLOCK
    qb = jnp.moveaxis(q.astype(jnp.float32).reshape(bn, nb, DIFF_BLOCK, h, 2, dq), 1, 0)
    kf = k.astype(jnp.float32)
    vf = v.astype(jnp.float32)
    kpos = jnp.arange(s)
    scale = dq ** -0.5

    def block(args):
        i, qi = args
        sc = jnp.einsum("bqhcd,bkhcd->bhcqk", qi, kf) * scale
        qpos = i * DIFF_BLOCK + jnp.arange(DIFF_BLOCK)
        causal = kpos[None, :] <= qpos[:, None]
        sc = jnp.where(causal[None, None, None], sc, NEG_INF)
        p = jax.nn.softmax(sc, axis=-1)
        a = p[:, :, 0] - lam * p[:, :, 1]
        return jnp.einsum("bhqk,bkhd->bqhd", a, vf)

    o = lax.map(block, (jnp.arange(nb), qb))
    o = jnp.moveaxis(o, 0, 1).reshape(bn, s, h, HEAD_DIM)
    o = rmsnorm(o, subln_g) * (1.0 - lam_init)
    return o.reshape(bn, s, h * HEAD_DIM).astype(v.dtype)


def setup_inputs(seed: int = 0) -> dict:
    key = jax.random.key(seed)
    ks = jax.random.split(key, 32)
    f32 = jnp.float32

    def nrm(k, shape, scale):
        return jax.random.normal(k, shape, f32) * scale

    def gain(k, shape):
        return 1.0 + 0.02 * jax.random.normal(k, shape, f32)

    x = jax.random.normal(ks[0], (BATCH, SEQ, D_MODEL), f32)
    offset = jax.random.randint(ks[1], (BATCH, 1), 0, 1024, dtype=jnp.int32)
    positions = offset + jnp.arange(SEQ, dtype=jnp.int32)[None, :]
    return {
        "x": x,
        "positions": positions,
        "ffn1_norm": gain(ks[2], (DEPTH, D_MODEL)),
        "ffn1_w13": nrm(ks[3], (DEPTH, D_MODEL, 2 * D_FF), D_MODEL ** -0.5),
        "ffn1_w2": nrm(ks[4], (DEPTH, D_FF, D_MODEL), D_FF ** -0.5),
        "mix_norm": gain(ks[5], (DEPTH, D_MODEL)),
        "w_in": nrm(ks[6], (DEPTH, D_MODEL, IN_COLS), D_MODEL ** -0.5),
        "conv_w": nrm(ks[7], (DEPTH, CONV_WIDTH, CONV_CH), CONV_WIDTH ** -0.5),
        "conv_b": nrm(ks[8], (DEPTH, CONV_CH), 0.02),
        "conv_ln_g": gain(ks[9], (DEPTH, CONV_CH)),
        "conv_ln_b": nrm(ks[10], (DEPTH, CONV_CH), 0.02),
        "sgu_ln_g": gain(ks[11], (DEPTH, SGU_CH)),
        "sgu_ln_b": nrm(ks[12], (DEPTH, SGU_CH), 0.02),
        "sgu_w": nrm(ks[13], (DEPTH, SGU_GROUPS, SGU_CHUNK, SGU_CHUNK), SGU_CHUNK ** -0.5),
        "sgu_b": gain(ks[14], (DEPTH, SGU_GROUPS, SGU_CHUNK)),
        "dil_q_norm": gain(ks[15], (DEPTH, HEAD_DIM)),
        "dil_k_norm": gain(ks[16], (DEPTH, HEAD_DIM)),
        "diff_q_norm": gain(ks[17], (DEPTH, DIFF_QK_DIM)),
        "diff_k_norm": gain(ks[18], (DEPTH, DIFF_QK_DIM)),
        "diff_lq1": nrm(ks[19], (DEPTH, DIFF_QK_DIM), 0.1),
        "diff_lk1": nrm(ks[20], (DEPTH, DIFF_QK_DIM), 0.1),
        "diff_lq2": nrm(ks[21], (DEPTH, DIFF_QK_DIM), 0.1),
        "diff_lk2": nrm(ks[22], (DEPTH, DIFF_QK_DIM), 0.1),
        "diff_subln": gain(ks[23], (DEPTH, HEAD_DIM)),
        "w_branch": nrm(ks[24], (DEPTH, N_BRANCHES, BRANCH_WIDTH, D_MODEL), BRANCH_WIDTH ** -0.5),
        "w_out": nrm(ks[25], (DEPTH, D_MODEL, D_MODEL), D_MODEL ** -0.5),
        "ffn2_norm": gain(ks[26], (DEPTH, D_MODEL)),
        "ffn2_w13": nrm(ks[27], (DEPTH, D_MODEL, 2 * D_FF), D_MODEL ** -0.5),
        "ffn2_w2": nrm(ks[28], (DEPTH, D_FF, D_MODEL), D_FF ** -0.5),
    }


def reference(x, positions, ffn1_norm, ffn1_w13, ffn1_w2, mix_norm, w_in, conv_w, conv_b,
              conv_ln_g, conv_ln_b, sgu_ln_g, sgu_ln_b, sgu_w, sgu_b, dil_q_norm, dil_k_norm,
              diff_q_norm, diff_k_norm, diff_lq1, diff_lk1, diff_lq2, diff_lk2, diff_subln,
              w_branch, w_out, ffn2_norm, ffn2_w13, ffn2_w2):
    bn, s, _ = x.shape
    for l in range(DEPTH):
        x = x + 0.5 * swiglu_ffn(x, ffn1_norm[l], ffn1_w13[l], ffn1_w2[l])

        h = rmsnorm(x, mix_norm[l])
        proj = h @ w_in[l]
        a_in, b_in, c_qkv, d_q, d_k, d_v, gates = jnp.split(proj, SPLIT_POINTS, axis=-1)

        ya = conformer_conv(a_in, conv_w[l], conv_b[l], conv_ln_g[l], conv_ln_b[l])

        yb = spatial_gating(b_in, sgu_ln_g[l], sgu_ln_b[l], sgu_w[l], sgu_b[l])

        cq, ck, cv = jnp.split(c_qkv.reshape(bn, s, 3, DIL_HEADS, HEAD_DIM), 3, axis=2)
        cq = partial_rope(rmsnorm(cq[:, :, 0], dil_q_norm[l]), positions)
        ck = partial_rope(rmsnorm(ck[:, :, 0], dil_k_norm[l]), positions)
        yc = dilated_window_attention(cq, ck, cv[:, :, 0])

        dq = partial_rope(rmsnorm(d_q.reshape(bn, s, DIFF_HEADS, 2, DIFF_QK_DIM), diff_q_norm[l]), positions)
        dk = partial_rope(rmsnorm(d_k.reshape(bn, s, DIFF_HEADS, 2, DIFF_QK_DIM), diff_k_norm[l]), positions)
        dv = d_v.reshape(bn, s, DIFF_HEADS, HEAD_DIM)
        lam_init = 0.8 - 0.6 * math.exp(-0.3 * l)
        lam = (jnp.exp(jnp.sum(diff_lq1[l].astype(jnp.float32) * diff_lk1[l].astype(jnp.float32)))
               - jnp.exp(jnp.sum(diff_lq2[l].astype(jnp.float32) * diff_lk2[l].astype(jnp.float32)))
               + lam_init)
        yd = diff_attention(dq, dk, dv, lam, lam_init, diff_subln[l])

        g = jax.nn.sigmoid(gates).reshape(bn, s, N_BRANCHES, D_MODEL)
        branches = (ya, yb, yc, yd)
        merged = g[:, :, 0] * (branches[0] @ w_branch[l, 0])
        for i in range(1, N_BRANCHES):
            merged = merged + g[:, :, i] * (branches[i] @ w_branch[l, i])
        x = x + merged @ w_out[l]

        x = x + 0.5 * swiglu_ffn(x, ffn2_norm[l], ffn2_w13[l], ffn2_w2[l])
    return x
```

```python
from contextlib import ExitStack
import numpy as np
import concourse.bass as bass
import concourse.mybir as mybir
from concourse.bass_utils import run_bass_kernel_spmd

F32 = mybir.dt.float32
BF16 = mybir.dt.bfloat16
I32 = mybir.dt.int32
AF = mybir.ActivationFunctionType
ALU = mybir.AluOpType
AX = mybir.AxisListType


class Buf:
    __slots__ = ("name", "w", "r")

    def __init__(self, name=""):
        self.name = name
        self.w = None
        self.r = {}


class KB:
    RINGN = {"sp": 8, "pool": 4}

    def __init__(self, nc, es: ExitStack):
        self.nc = nc
        self.es = es
        self.eng = {"pe": nc.tensor, "act": nc.scalar, "dve": nc.vector,
                    "pool": nc.gpsimd, "sp": nc.sync}
        self.sems = {}
        self.cnt = {}
        self.k = {}
        self.seen = {e: {} for e in self.eng}
        self.ring_i = {}
        self.ring_val = {}
        self.epoch = -1
        self.n_inst = 0
        self.new_epoch()

    def new_epoch(self):
        self.epoch += 1
        ep = self.epoch
        for e in ("pe", "act", "dve", "pool"):
            self.k[e] = (e, ep)
            self.sems[self.k[e]] = self.es.enter_context(self.nc.semaphore("s_%s_%d" % (e, ep)))
            self.cnt[e] = 0
        self.ring_val = {}
        for q, n in self.RINGN.items():
            self.ring_i[q] = 0
            for s in range(n):
                k = ("ring", q, s, ep)
                self.sems[k] = self.es.enter_context(self.nc.semaphore("r_%s%d_%d" % (q, s, ep)))
                self.ring_val[k] = 0

    def sb(self, name, shape, dtype):
        return self.es.enter_context(self.nc.sbuf_tensor(name, list(shape), dtype))

    def ps(self, name, shape=(128, 512), dtype=F32):
        return self.es.enter_context(self.nc.psum_tensor(name, list(shape), dtype))

    def dram(self, name, shape, dtype, kind="Internal"):
        return self.nc.dram_tensor(name, list(shape), dtype, kind=kind).ap()

    def _need(self, e, tok):
        sk, val = tok
        if self.seen[e].get(sk, 0) >= val:
            return
        self.eng[e].wait_ge(self.sems[sk], val)
        self.n_inst += 1
        self.seen[e][sk] = val

    def _sync(self, e, reads, writes):
        me = self.k.get(e)
        for b in reads:
            if b.w is not None:
                self._need(e, b.w)
        for b in writes:
            if b.w is not None and b.w[0] != me:
                self._need(e, b.w)
            for sk, val in b.r.items():
                if sk != me or e != "pe":
                    self._need(e, (sk, val))

    def _mark(self, tok, reads, writes):
        sk, val = tok
        for b in reads:
            b.r[sk] = val
        for b in writes:
            b.w = tok
            b.r = {}

    def op(self, e, fn, reads=(), writes=(), sig=True):
        self._sync(e, reads, writes)
        ins = fn(self.eng[e])
        self.n_inst += 1
        if sig:
            self.cnt[e] += 1
            ins.then_inc(self.sems[self.k[e]], 1)
            tok = (self.k[e], self.cnt[e])
        else:
            tok = (self.k[e], self.cnt[e] + 1)
        self._mark(tok, reads, writes)
        return ins

    def dma(self, q, out, in_, reads=(), writes=(), **kw):
        i = self.ring_i[q]
        self.ring_i[q] = (i + 1) % self.RINGN[q]
        sk = ("ring", q, i, self.epoch)
        if self.ring_val[sk] > 0:
            self._need(q, (sk, self.ring_val[sk]))
        self._sync(q, reads, writes)
        ins = self.eng[q].dma_start(out=out, in_=in_, **kw)
        self.n_inst += 1
        self.ring_val[sk] += 16
        ins.then_inc(self.sems[sk], 16)
        self._mark((sk, self.ring_val[sk]), reads, writes)
        return ins

    def barrier(self):
        for e in ("pe", "act", "dve", "pool", "sp"):
            for k in ("pe", "act", "dve", "pool"):
                if k != e and self.cnt[k] > 0:
                    self._need(e, (self.k[k], self.cnt[k]))
            for sk, v in self.ring_val.items():
                if v > 0:
                    self._need(e, (sk, v))

    def finish(self, bufs, e="sp"):
        for b in bufs:
            if b.w is not None:
                self._need(e, b.w)
        for k in ("pe", "act", "dve", "pool"):
            if self.cnt[k] > 0:
                self._need(e, (self.k[k], self.cnt[k]))
        for sk, v in self.ring_val.items():
            if v > 0:
                self._need(e, (sk, v))


D = 2048
T = 2048
DFF = 5632
KC = D // 128
NJ = DFF // 128
EPS = 1e-6

C_A, C_AG, C_U, C_V = 0, 512, 1024, 1536
C_CQ, C_CK, C_CV = 2048, 2560, 3072
C_DQ, C_DK, C_DV = 3584, 4096, 4608
C_G = 5120
INC = 13312
PP_G1, PP_GM, PP_G2, PP_QKG = 0, 16, 32, 48
PP_CW, PP_CB, PP_CG, PP_CBE = 52, 176, 180, 184
PP_SUB, PP_LI, PP_OML, PP_LAM = 188, 189, 190, 191
PP_SG, PP_SB, PP_SW, PP_SBS = 447, 959, 1471, 1983
NPP = 2495
CC_INVF, CC_FLAG = 0, 2
CC_RC, CC_RD, CC_O128, CC_O64, CC_M0, CC_M1 = 4, 132, 260, 388, 516, 644
CC_DM = 772
NCC = 772 + 2048
NEG = -30000.0


class Scope:
    cnt = 0

    def __init__(self, P):
        self.P = P

    def __enter__(self):
        self.es = ExitStack()
        self.es.__enter__()
        return self

    def sb(self, name, shape, dtype):
        Scope.cnt += 1
        return self.es.enter_context(self.P.nc.sbuf_tensor("%s_%d" % (name, Scope.cnt), list(shape), dtype))

    def __exit__(self, *a):
        self.P.kb.barrier()
        return self.es.__exit__(*a)


class Prog:
    def __init__(self, nc, es, pp_dram, cc_dram, pos_dram):
        self.nc = nc
        self.kb = KB(nc, es)
        kb = self.kb
        self.psum = [kb.ps("ps%d" % i) for i in range(8)]
        self.psum_b = [Buf("ps%d" % i) for i in range(8)]
        self.ps_i = 0
        self.pp = kb.sb("pp_sb", [128, NPP], F32)
        self.pp_b = Buf("pp")
        kb.dma("sp", self.pp[:, :], pp_dram, writes=[self.pp_b])
        self.cb = Buf("consts")
        ccf = kb.sb("ccf", [128, NCC], F32)
        self.ccf = ccf
        kb.dma("sp", ccf[:, :], cc_dram, writes=[self.cb])
        self.ccb = kb.sb("ccb", [128, NCC], BF16)
        kb.op("dve", lambda e: e.tensor_copy(out=self.ccb[:, :], in_=ccf[:, :]), reads=[self.cb], writes=[self.cb])
        self.onesD = kb.sb("onesD", [128, 128], BF16)
        kb.op("dve", lambda e: e.memset(self.onesD[:, :], 1.0 / D), writes=[self.cb])
        self.ones1 = kb.sb("ones1", [128, 128], BF16)
        kb.op("dve", lambda e: e.memset(self.ones1[:, :], 1.0), writes=[self.cb])
        self.onesLN = kb.sb("onesLN", [128, 128], F32)
        kb.op("dve", lambda e: e.memset(self.onesLN[:, :], 1.0 / 512), writes=[self.cb])
        self.epsc = kb.sb("epsc", [128, 1], F32)
        kb.op("dve", lambda e: e.memset(self.epsc[:, :], EPS), writes=[self.cb])
        self.cs = kb.sb("cs", [128, 4, T], F32)
        with Scope(self) as sc:
            posi = sc.sb("posi", [128, T], I32)
            posf = sc.sb("posf", [128, T], F32)
            ang = sc.sb("ang", [128, T], F32)
            kf = sc.sb("kf", [128, T], F32)
            ki = sc.sb("ki", [128, T], I32)
            b = Buf("cs_tmp")
            TWO_PI = float(2 * np.pi)
            kb.dma("sp", posi[:, :], pos_dram, writes=[b])
            kb.op("dve", lambda e: e.tensor_copy(out=posf[:, :], in_=posi[:, :]), reads=[b], writes=[b])
            for ty in range(2):
                for cs_i, shift in ((0, 0.5 * np.pi), (1, 0.0)):
                    dst = self.cs[:, 2 * ty + cs_i, :]
                    kb.op("dve", lambda e: e.tensor_scalar(out=ang[:, :], in0=posf[:, :], scalar1=ccf[:, CC_INVF + ty:CC_INVF + ty + 1],
                                                           scalar2=float(shift), op0=ALU.mult, op1=ALU.add),
                          reads=[b, self.cb], writes=[b])
                    kb.op("dve", lambda e: e.tensor_scalar(out=kf[:, :], in0=ang[:, :], scalar1=1.0 / TWO_PI, scalar2=None, op0=ALU.mult),
                          reads=[b], writes=[b])
                    kb.op("dve", lambda e: e.tensor_copy(out=ki[:, :], in_=kf[:, :]), reads=[b], writes=[b])
                    kb.op("dve", lambda e: e.tensor_copy(out=kf[:, :], in_=ki[:, :]), reads=[b], writes=[b])
                    kb.op("dve", lambda e: e.scalar_tensor_tensor(out=ang[:, :], in0=kf[:, :], scalar=-TWO_PI, in1=ang[:, :],
                                                                  op0=ALU.mult, op1=ALU.add), reads=[b], writes=[b])
                    kb.op("dve", lambda e: e.tensor_scalar(out=kf[:, :], in0=ang[:, :], scalar1=float(np.pi), scalar2=-TWO_PI,
                                                           op0=ALU.is_gt, op1=ALU.mult), reads=[b], writes=[b])
                    kb.op("dve", lambda e: e.tensor_tensor(out=ang[:, :], in0=ang[:, :], in1=kf[:, :], op=ALU.add), reads=[b], writes=[b])
                    kb.op("act", lambda e: e.activation(out=dst, in_=ang[:, :], func=AF.Sin), reads=[b], writes=[self.cb])

    def next_ps(self):
        i = self.ps_i
        self.ps_i = (i + 1) % 8
        return self.psum[i], self.psum_b[i]


def hT_view(big, k, t0, t1):
    return big[:, k * T + t0: k * T + t1]


def norm_phase(P, sc, big, big_b, xT, gcol0):
    with P.nc.named_scope("norm"):
        _norm_phase(P, sc, big, big_b, xT, gcol0)


def _norm_phase(P, sc, big, big_b, xT, gcol0):
    kb = P.kb
    xins = [sc.sb("xin", [128, KC, 256], F32) for _ in range(2)]; xins_b = [Buf() for _ in range(2)]
    sq = sc.sb("sq", [128, KC, 256], BF16); sq_b = Buf()
    rstd = sc.sb("rstd", [128, 256], F32); rstd_b = Buf()
    xv = xT.rearrange("(k p) t -> p k t", p=128)
    for t8 in range(8):
        t0, t1 = t8 * 256, (t8 + 1) * 256
        tt = t8 // 2
        xin, xin_b = xins[t8 % 2], xins_b[t8 % 2]
        kb.dma("sp", xin[:, :, :], xv[:, :, t0:t1], writes=[xin_b])
        kb.op("act", lambda e: e.activation(out=sq[:, :, :], in_=xin[:, :, :], func=AF.Square), reads=[xin_b], writes=[sq_b])
        ps, ps_b = P.next_ps()
        for k in range(KC):
            kb.op("pe", lambda e: e.matmul(ps[:, 0:256], lhsT=P.onesD[:, :], rhs=sq[:, k, :], start=(k == 0), stop=(k == KC - 1)),
                  reads=[P.cb, sq_b], writes=[ps_b], sig=(k == KC - 1))
        kb.op("act", lambda e: e.activation(out=rstd[:, :], in_=ps[:, 0:256], func=AF.Sqrt, bias=P.epsc[:, :], scale=1.0),
              reads=[ps_b, P.cb], writes=[rstd_b])
        kb.op("dve", lambda e: e.reciprocal(out=rstd[:, :], in_=rstd[:, :]), reads=[rstd_b], writes=[rstd_b])
        for k in range(KC):
            kb.op("dve", lambda e: e.scalar_tensor_tensor(out=hT_view(big, k, t0, t1), in0=xin[:, k, :],
                                                          scalar=P.pp[:, gcol0 + k:gcol0 + k + 1], in1=rstd[:, :],
                                                          op0=ALU.mult, op1=ALU.mult),
                  reads=[xin_b, P.pp_b, rstd_b], writes=[big_b[tt]])


def ffn_phase(P, xT_in, xT_out, gcol0, w13, w2, actT, out_b):
    with P.nc.named_scope("ffn%d" % gcol0):
        _ffn_phase(P, xT_in, xT_out, gcol0, w13, w2, actT, out_b)


def _ffn_phase(P, xT_in, xT_out, gcol0, w13, w2, actT, out_b):
    kb = P.kb
    actT_b = Buf("actT")
    with Scope(P) as sc:
        big = sc.sb("big", [128, NJ * 1024], BF16)
        big_b = [Buf() for _ in range(4)]
        bigA_bs = [Buf() for _ in range(4)]
        with Scope(P) as sc2:
            norm_phase(P, sc2, big, big_b, xT_in, gcol0)
        wa = [sc.sb("wa", [128, 2, KC, 128], BF16) for _ in range(2)]; wa_b = [Buf() for _ in range(2)]
        w2s = [sc.sb("w2s", [128, NJ, 128], BF16) for _ in range(2)]; w2s_b = [Buf() for _ in range(2)]
        sg = [sc.sb("sg", [128, 512], BF16) for _ in range(2)]; sg_b = [Buf() for _ in range(2)]
        actt = [sc.sb("actt", [128, T], BF16) for _ in range(2)]; actt_b = [Buf() for _ in range(2)]
        xr = [sc.sb("xr", [128, 1024], F32) for _ in range(2)]; xr_b = [Buf() for _ in range(2)]
        w13v = w13.rearrange("(k p) c -> p k c", p=128)
        for j in range(NJ):
            w, w_b = wa[j % 2], wa_b[j % 2]
            kb.dma("pool", w[:, 0, :, :], w13v[:, :, j * 128:(j + 1) * 128], writes=[w_b])
            kb.dma("pool", w[:, 1, :, :], w13v[:, :, DFF + j * 128: DFF + (j + 1) * 128], writes=[w_b])
            at, at_b = actt[j % 2], actt_b[j % 2]
            for tt in range(4):
                t0, t1 = tt * 512, (tt + 1) * 512
                pg, pg_b = P.next_ps()
                pu, pu_b = P.next_ps()
                for k in range(KC):
                    kb.op("pe", lambda e: e.matmul(pg[:, :], lhsT=w[:, 0, k, :], rhs=hT_view(big, k, t0, t1),
                                                   start=(k == 0), stop=(k == KC - 1)),
                          reads=[w_b, big_b[tt]], writes=[pg_b], sig=(k == KC - 1))
                for k in range(KC):
                    kb.op("pe", lambda e: e.matmul(pu[:, :], lhsT=w[:, 1, k, :], rhs=hT_view(big, k, t0, t1),
                                                   start=(k == 0), stop=(k == KC - 1)),
                          reads=[w_b, big_b[tt]], writes=[pu_b], sig=(k == KC - 1))
                s_, s_b = sg[tt % 2], sg_b[tt % 2]
                kb.op("act", lambda e: e.activation(out=s_[:, :], in_=pg[:, :], func=AF.Silu), reads=[pg_b], writes=[s_b])
                kb.op("dve", lambda e: e.tensor_tensor(out=at[:, t0:t1], in0=pu[:, :], in1=s_[:, :], op=ALU.mult),
                      reads=[pu_b, s_b], writes=[at_b])
            kb.dma("sp", actT[j * 128:(j + 1) * 128, :], at[:, :], reads=[at_b], writes=[actT_b])
        actv = actT.rearrange("(j p) t -> p j t", p=128)
        w2v = w2.rearrange("(j p) c -> p j c", p=128)
        A = big
        for th in range(2):
            h0 = th * 1024
            for jj in range(0, NJ, 11):
                kb.dma("sp", A[:, jj * 1024:(jj + 11) * 1024].rearrange("p (j t) -> p j t", t=1024),
                       actv[:, jj:jj + 11, h0:h0 + 1024], reads=[actT_b], writes=[bigA_bs[jj // 11]] + big_b)
            for m in range(KC):
                ws, ws_b = w2s[m % 2], w2s_b[m % 2]
                for jj in range(0, NJ, 11):
                    kb.dma("pool", ws[:, jj:jj + 11, :], w2v[:, jj:jj + 11, m * 128:(m + 1) * 128], writes=[ws_b])
                x_, x_b = xr[m % 2], xr_b[m % 2]
                kb.dma("sp", x_[:, :], xT_in[m * 128:(m + 1) * 128, h0:h0 + 1024], writes=[x_b])
                p0, p0_b = P.next_ps()
                p1, p1_b = P.next_ps()
                for j in range(NJ):
                    kb.op("pe", lambda e: e.matmul(p0[:, :], lhsT=ws[:, j, :], rhs=A[:, j * 1024: j * 1024 + 512],
                                                   start=(j == 0), stop=(j == NJ - 1)),
                          reads=[ws_b, bigA_bs[j // 11]], writes=[p0_b], sig=(j == NJ - 1))
                for j in range(NJ):
                    kb.op("pe", lambda e: e.matmul(p1[:, :], lhsT=ws[:, j, :], rhs=A[:, j * 1024 + 512: (j + 1) * 1024],
                                                   start=(j == 0), stop=(j == NJ - 1)),
                          reads=[ws_b, bigA_bs[j // 11]], writes=[p1_b], sig=(j == NJ - 1))
                o_, o_b = x_, x_b
                kb.op("dve", lambda e: e.scalar_tensor_tensor(out=o_[:, 0:512], in0=p0[:, :], scalar=0.5, in1=x_[:, 0:512],
                                                              op0=ALU.mult, op1=ALU.add), reads=[p0_b, x_b], writes=[o_b])
                kb.op("dve", lambda e: e.scalar_tensor_tensor(out=o_[:, 512:1024], in0=p1[:, :], scalar=0.5, in1=x_[:, 512:1024],
                                                              op0=ALU.mult, op1=ALU.add), reads=[p1_b, x_b], writes=[o_b])
                kb.dma("sp", xT_out[m * 128:(m + 1) * 128, h0:h0 + 1024], o_[:, :], reads=[o_b], writes=[out_b])


def proj_fm(P, big, big_b, wv, cols, evac, wpool, wpool_b):
    kb = P.kb
    for ci, c0 in enumerate(cols):
        w, w_b = wpool[ci % len(wpool)], wpool_b[ci % len(wpool)]
        kb.dma("pool", w[:, :, :], wv[:, :, c0:c0 + 128], writes=[w_b])
        for tt in range(4):
            t0, t1 = tt * 512, (tt + 1) * 512
            ps, ps_b = P.next_ps()
            for k in range(KC):
                kb.op("pe", lambda e: e.matmul(ps[:, :], lhsT=w[:, k, :], rhs=hT_view(big, k, t0, t1),
                                               start=(k == 0), stop=(k == KC - 1)),
                      reads=[w_b, big_b[tt]], writes=[ps_b], sig=(k == KC - 1))
            evac(ci, tt, ps, ps_b)


def proj_tm(P, sc, big, big_b, wv, c0, evac):
    kb = P.kb
    ws = sc.sb("wtm", [128, KC, 512], BF16); ws_b = Buf()
    for k4 in range(0, KC, 4):
        for cc in range(0, 512, 128):
            kb.dma("pool", ws[:, k4:k4 + 4, cc:cc + 128], wv[:, k4:k4 + 4, c0 + cc:c0 + cc + 128], writes=[ws_b])
    for i in range(16):
        ps, ps_b = P.next_ps()
        for k in range(KC):
            kb.op("pe", lambda e: e.matmul(ps[:, :], lhsT=hT_view(big, k, i * 128, (i + 1) * 128), rhs=ws[:, k, :],
                                           start=(k == 0), stop=(k == KC - 1)),
                  reads=[ws_b, big_b[i // 4]], writes=[ps_b], sig=(k == KC - 1))
        evac(i, ps, ps_b)


def qk_evac_factory(P, sc, ty, gcol, out_dram):
    kb = P.kb
    NB = 3
    raw = [sc.sb("qraw", [128, 512], F32) for _ in range(NB)]; raw_b = [Buf() for _ in range(NB)]
    sq = [sc.sb("qsq", [128, 512], BF16) for _ in range(NB)]; sq_b = [Buf() for _ in range(NB)]
    rs = [sc.sb("qrs", [128, 512], F32) for _ in range(NB)]; rs_b = [Buf() for _ in range(NB)]
    kn = [sc.sb("qkn", [128, 512], BF16) for _ in range(NB)]; kn_b = [Buf() for _ in range(NB)]
    t1 = [sc.sb("qt1", [128, 512], F32) for _ in range(NB)]; t1_b = [Buf() for _ in range(NB)]
    ot = [sc.sb("qot", [128, T], BF16) for _ in range(2)]; ot_b = [Buf() for _ in range(2)]
    ones_g = P.ccb[:, CC_O128:CC_O128 + 128] if ty == 0 else P.ccb[:, CC_O64:CC_O64 + 128]
    Rm = P.ccb[:, CC_RC:CC_RC + 128] if ty == 0 else P.ccb[:, CC_RD:CC_RD + 128]
    state = {"n": 0}
    ob = Buf()
    q1, q2 = [], []

    def stage1(it):
        ci, tt, ps, ps_b, i = it
        kb.op("act", lambda e: e.activation(out=raw[i][:, :], in_=ps[:, :], func=AF.Copy), reads=[ps_b], writes=[raw_b[i]])
        kb.op("act", lambda e: e.activation(out=sq[i][:, :], in_=ps[:, :], func=AF.Square), reads=[ps_b], writes=[sq_b[i]])
        p2, p2_b = P.next_ps()
        kb.op("pe", lambda e: e.matmul(p2[:, :], lhsT=ones_g, rhs=sq[i][:, :], start=True, stop=True),
              reads=[P.cb, sq_b[i]], writes=[p2_b])
        kb.op("act", lambda e: e.activation(out=rs[i][:, :], in_=p2[:, :], func=AF.Sqrt, bias=P.epsc[:, :], scale=1.0),
              reads=[p2_b, P.cb], writes=[rs_b[i]])
        kb.op("dve", lambda e: e.reciprocal(out=rs[i][:, :], in_=rs[i][:, :]), reads=[rs_b[i]], writes=[rs_b[i]])
        kb.op("dve", lambda e: e.scalar_tensor_tensor(out=kn[i][:, :], in0=raw[i][:, :], scalar=P.pp[:, gcol:gcol + 1],
                                                      in1=rs[i][:, :], op0=ALU.mult, op1=ALU.mult),
              reads=[raw_b[i], rs_b[i], P.pp_b], writes=[kn_b[i]])

    def stage2(it):
        ci, tt, ps, ps_b, i = it
        t0, t1_ = tt * 512, (tt + 1) * 512
        o, o_b = ot[ci % 2], ot_b[ci % 2]
        p3, p3_b = P.next_ps()
        kb.op("pe", lambda e: e.matmul(p3[:, :], lhsT=Rm, rhs=kn[i][:, :], start=True, stop=True),
              reads=[P.cb, kn_b[i]], writes=[p3_b])
        kb.op("dve", lambda e: e.tensor_tensor(out=t1[i][:, :], in0=kn[i][:, :], in1=P.cs[:, 2 * ty, t0:t1_], op=ALU.mult),
              reads=[kn_b[i], P.cb], writes=[t1_b[i]])
        kb.op("dve", lambda e: e.tensor_tensor(out=rs[i][:, :], in0=p3[:, :], in1=P.cs[:, 2 * ty + 1, t0:t1_], op=ALU.mult),
              reads=[p3_b, P.cb], writes=[rs_b[i]])
        kb.op("pool", lambda e: e.tensor_tensor(out=o[:, t0:t1_], in0=t1[i][:, :], in1=rs[i][:, :], op=ALU.add),
              reads=[t1_b[i], rs_b[i]], writes=[o_b])
        if tt == 3:
            kb.dma("sp", out_dram[ci * 128:(ci + 1) * 128, :], o[:, :], reads=[o_b], writes=[ob])

    def step():
        if q2:
            stage2(q2.pop(0))
        if q1:
            it = q1.pop(0)
            stage1(it)
            q2.append(it)

    def evac(ci, tt, ps, ps_b):
        n = state["n"]; state["n"] += 1
        step()
        q1.append((ci, tt, ps, ps_b, n % NB))

    def flush():
        while q1 or q2:
            step()
    return evac, flush, ob


def launchA_body(P, d):
    kb = P.kb
    out_b = Buf("x1")
    ffn_phase(P, d["xT"], d["x1T"], PP_G1, d["w13a"], d["w2a"], d["actT"], out_b)
    kb.barrier()
    exp_b = Buf("exports")
    wv = d["w_in"].rearrange("(k p) c -> p k c", p=128)
    _sA = P.nc.enter_named_scope("pA", False)[0]
    with Scope(P) as sc:
        big = sc.sb("big", [128, KC * T], BF16)
        big_b = [Buf() for _ in range(4)]
        with Scope(P) as sc2:
            norm_phase(P, sc2, big, big_b, d["x1T"], PP_GM)
        wpool = [sc.sb("wp", [128, KC, 128], BF16) for _ in range(3)]; wpool_b = [Buf() for _ in range(3)]
        sc_outer = sc
        sc = Scope(P); sc.__enter__()
        a_sb = sc.sb("a_sb", [128, T], F32); a_b = Buf()
        sgt = [sc.sb("sgt", [128, 512], F32) for _ in range(2)]; sgt_b = [Buf() for _ in range(2)]
        z_sb = [sc.sb("z_sb", [128, T], F32) for _ in range(2)]; z_b = [Buf() for _ in range(2)]

        def evac_z(ci, tt, ps, ps_b):
            t0, t1 = tt * 512, (tt + 1) * 512
            c = ci // 2
            if ci % 2 == 0:
                kb.op("act", lambda e: e.activation(out=a_sb[:, t0:t1], in_=ps[:, :], func=AF.Copy), reads=[ps_b], writes=[a_b])
            else:
                s_, s_b = sgt[tt % 2], sgt_b[tt % 2]
                kb.op("act", lambda e: e.activation(out=s_[:, :], in_=ps[:, :], func=AF.Sigmoid), reads=[ps_b], writes=[s_b])
                kb.op("dve", lambda e: e.tensor_tensor(out=z_sb[c % 2][:, t0:t1], in0=a_sb[:, t0:t1], in1=s_[:, :], op=ALU.mult),
                      reads=[a_b, s_b], writes=[z_b[c % 2]])
                if tt == 3:
                    kb.dma("sp", d["zT"][c * 128:(c + 1) * 128, :], z_sb[c % 2][:, :], reads=[z_b[c % 2]], writes=[exp_b])
                    kb.dma("sp", d["halo"][c * 128:(c + 1) * 128, :], z_sb[c % 2][:, T - 32:T], reads=[z_b[c % 2]], writes=[exp_b])
        cols = []
        for c in range(4):
            cols += [C_A + c * 128, C_AG + c * 128]
        proj_fm(P, big, big_b, wv, cols, evac_z, wpool, wpool_b)
        sc.__exit__(None, None, None)
        sc = Scope(P); sc.__enter__()
        ev, fl, ob1 = qk_evac_factory(P, sc, 0, PP_QKG + 1, d["kCT"])
        proj_fm(P, big, big_b, wv, [C_CK + h * 128 for h in range(4)], ev, wpool, wpool_b)
        fl()
        sc.__exit__(None, None, None)
        sc = Scope(P); sc.__enter__()
        ev, fl, ob2 = qk_evac_factory(P, sc, 1, PP_QKG + 3, d["kDT"])
        proj_fm(P, big, big_b, wv, [C_DK + h * 128 for h in range(4)], ev, wpool, wpool_b)
        fl()
        sc.__exit__(None, None, None)
        sc = Scope(P); sc.__enter__()
        vt = [sc.sb("vt", [128, 512], BF16) for _ in range(2)]; vt_b = [Buf() for _ in range(2)]
        for c0, name in ((C_CV, "vC"), (C_DV, "vD")):
            def evac_v(i, ps, ps_b, name=name):
                kb.op("act", lambda e: e.activation(out=vt[i % 2][:, :], in_=ps[:, :], func=AF.Copy), reads=[ps_b], writes=[vt_b[i % 2]])
                kb.dma("sp", d[name][i * 128:(i + 1) * 128, :], vt[i % 2][:, :], reads=[vt_b[i % 2]], writes=[exp_b])
            proj_tm(P, sc, big, big_b, wv, c0, evac_v)
        sc.__exit__(None, None, None)
        P.nc.leave_named_scope("pA", _sA, False)
        if "after_exports" in d:
            d["after_exports"]()
        if "projB" in d:
            d["projB"](big, big_b, wv, wpool, wpool_b)
    return [out_b, exp_b, ob1, ob2]


def launchB_body(P, d, after_phase1=None):
    kb = P.kb
    nc = P.nc
    wv = d["w_in"].rearrange("(k p) c -> p k c", p=128)
    qCT, qDT, gT, yT, x2T = d["qCT"], d["qDT"], d["gT"], d["yT"], d["x2T"]
    sb_ = Buf("scratch")
    ccf, ccb, pp = P.ccf, P.ccb, P.pp
    flagb = ccf[:, CC_FLAG + 1:CC_FLAG + 2]
    _cur = ['p1proj', nc.enter_named_scope('p1proj', False)[0]]
    def conv_main(sc):
        zb = [sc.sb("zb", [128, 32 + T], F32) for _ in range(2)]; zb_b = [Buf() for _ in range(2)]
        acc = sc.sb("cacc", [128, 4, T], F32); acc_b = [Buf() for _ in range(4)]
        for c in range(4):
            z, z_b = zb[c % 2], zb_b[c % 2]
            kb.dma("sp", z[:, 30:30 + T], d["zT"][c * 128:(c + 1) * 128, :], writes=[z_b])
            kb.op("dve", lambda e: e.memset(z[:, 0:30], 0.0), writes=[z_b])
            wc = PP_CW + c * 31
            kb.op("dve", lambda e: e.tensor_scalar(out=acc[:, c, :], in0=z[:, 0:T], scalar1=pp[:, wc:wc + 1], scalar2=pp[:, PP_CB + c:PP_CB + c + 1],
                                                   op0=ALU.mult, op1=ALU.add), reads=[z_b, P.pp_b], writes=[acc_b[c]])
            for k in range(1, 31):
                kb.op("dve", lambda e: e.scalar_tensor_tensor(out=acc[:, c, :], in0=z[:, k:k + T], scalar=pp[:, wc + k:wc + k + 1],
                                                              in1=acc[:, c, :], op0=ALU.mult, op1=ALU.add),
                      reads=[z_b, P.pp_b, acc_b[c]], writes=[acc_b[c]])
        return acc, acc_b

    def conv_finish(sc, cst):
        acc, acc_b = cst
        zh = sc.sb("zh", [128, 4, 64], F32); zh_b = Buf()
        cor = sc.sb("ccor", [128, 4, 32], F32); cor_b = [Buf() for _ in range(4)]
        kb.op("dve", lambda e: e.memset(zh[:, :, :], 0.0), writes=[zh_b])
        for c in range(4):
            kb.dma("sp", zh[:, c, 0:30], d["phalo"][c * 128:(c + 1) * 128, 2:32], writes=[zh_b])
        kb.op("dve", lambda e: e.tensor_scalar(out=zh[:, :, 0:30], in0=zh[:, :, 0:30], scalar1=ccf[:, CC_FLAG:CC_FLAG + 1], scalar2=None,
                                               op0=ALU.mult), reads=[zh_b, P.cb], writes=[zh_b])
        for c in range(4):
            wc = PP_CW + c * 31
            kb.op("dve", lambda e: e.tensor_scalar(out=cor[:, c, 0:30], in0=zh[:, c, 0:30], scalar1=pp[:, wc:wc + 1], scalar2=None,
                                                   op0=ALU.mult), reads=[zh_b, P.pp_b], writes=[cor_b[c]])
            for k in range(1, 30):
                kb.op("dve", lambda e: e.scalar_tensor_tensor(out=cor[:, c, 0:30], in0=zh[:, c, k:k + 30], scalar=pp[:, wc + k:wc + k + 1],
                                                              in1=cor[:, c, 0:30], op0=ALU.mult, op1=ALU.add),
                      reads=[zh_b, P.pp_b, cor_b[c]], writes=[cor_b[c]])
            kb.op("dve", lambda e: e.tensor_tensor(out=acc[:, c, 0:30], in0=acc[:, c, 0:30], in1=cor[:, c, 0:30], op=ALU.add),
                  reads=[cor_b[c], acc_b[c]], writes=[acc_b[c]])
        sqc = [sc.sb("sqc", [128, 512], F32) for _ in range(2)]; sqc_b = [Buf() for _ in range(2)]
        mean = sc.sb("cmean", [128, 512], F32); mean_b = Buf()
        var = sc.sb("cvar", [128, 512], F32); var_b = Buf()
        tq = [sc.sb("ctq", [128, 512], F32) for _ in range(2)]; tq_b = [Buf() for _ in range(2)]
        yat = [sc.sb("yat", [128, 512], BF16) for _ in range(2)]; yat_b = [Buf() for _ in range(2)]
        for tt in range(4):
            t0, t1 = tt * 512, (tt + 1) * 512
            pmn, pmn_b = P.next_ps()
            for c in range(4):
                kb.op("pe", lambda e: e.matmul(pmn[:, :], lhsT=P.onesLN[:, :], rhs=acc[:, c, t0:t1], start=(c == 0), stop=(c == 3)),
                      reads=[P.cb, acc_b[c]], writes=[pmn_b], sig=(c == 3))
            pex, pex_b = P.next_ps()
            for c in range(4):
                kb.op("act", lambda e: e.activation(out=sqc[c % 2][:, :], in_=acc[:, c, t0:t1], func=AF.Square),
                      reads=[acc_b[c]], writes=[sqc_b[c % 2]])
                kb.op("pe", lambda e: e.matmul(pex[:, :], lhsT=P.onesLN[:, :], rhs=sqc[c % 2][:, :], start=(c == 0), stop=(c == 3)),
                      reads=[P.cb, sqc_b[c % 2]], writes=[pex_b], sig=True)
            kb.op("act", lambda e: e.activation(out=mean[:, :], in_=pmn[:, :], func=AF.Copy), reads=[pmn_b], writes=[mean_b])
            kb.op("dve", lambda e: e.tensor_tensor(out=var[:, :], in0=mean[:, :], in1=mean[:, :], op=ALU.mult), reads=[mean_b], writes=[var_b])
            kb.op("dve", lambda e: e.tensor_tensor(out=var[:, :], in0=pex[:, :], in1=var[:, :], op=ALU.subtract), reads=[pex_b, var_b], writes=[var_b])
            kb.op("act", lambda e: e.activation(out=var[:, :], in_=var[:, :], func=AF.Sqrt, bias=P.epsc[:, :], scale=1.0),
                  reads=[var_b, P.cb], writes=[var_b])
            kb.op("dve", lambda e: e.reciprocal(out=var[:, :], in_=var[:, :]), reads=[var_b], writes=[var_b])
            for c in range(4):
                q_, q_b = tq[c % 2], tq_b[c % 2]
                kb.op("pool", lambda e: e.tensor_tensor(out=q_[:, :], in0=acc[:, c, t0:t1], in1=mean[:, :], op=ALU.subtract),
                      reads=[acc_b[c], mean_b], writes=[q_b])
                kb.op("dve", lambda e: e.tensor_tensor(out=q_[:, :], in0=q_[:, :], in1=var[:, :], op=ALU.mult), reads=[q_b, var_b], writes=[q_b])
                y_, y_b = yat[c % 2], yat_b[c % 2]
                kb.op("act", lambda e: e.activation(out=y_[:, :], in_=q_[:, :], func=AF.Silu, scale=pp[:, PP_CG + c:PP_CG + c + 1],
                                                    bias=pp[:, PP_CBE + c:PP_CBE + c + 1]), reads=[q_b, P.pp_b], writes=[y_b])
                kb.dma("sp", yT[c * 128:(c + 1) * 128, t0:t1], y_[:, :], reads=[y_b], writes=[sb_])

    def projB(big, big_b, wv, wpool, wpool_b):
        with P.nc.named_scope("pB"):
            _projB(big, big_b, wv, wpool, wpool_b)

    def _projB(big, big_b, wv, wpool, wpool_b):
        with Scope(P) as sc:
            ev, fl, ob = qk_evac_factory(P, sc, 0, PP_QKG + 0, qCT)
            proj_fm(P, big, big_b, wv, [C_CQ + h * 128 for h in range(4)], ev, wpool, wpool_b)
            fl()
        with Scope(P) as sc:
            ev, fl, ob = qk_evac_factory(P, sc, 1, PP_QKG + 2, qDT)
            proj_fm(P, big, big_b, wv, [C_DQ + h * 128 for h in range(4)], ev, wpool, wpool_b)
            fl()
        _sg = P.nc.enter_named_scope("pBgate", False)[0]
        with Scope(P) as sc:
            cst = conv_main(sc) if d.get("conv_in_proj") else None
            gt = [sc.sb("gt", [128, T], BF16) for _ in range(2)]; gt_b = [Buf() for _ in range(2)]

            def evac_g(ci, tt, ps, ps_b):
                t0, t1 = tt * 512, (tt + 1) * 512
                kb.op("act", lambda e: e.activation(out=gt[ci % 2][:, t0:t1], in_=ps[:, :], func=AF.Sigmoid),
                      reads=[ps_b], writes=[gt_b[ci % 2]])
                if tt == 3:
                    kb.dma("sp", gT[ci * 128:(ci + 1) * 128, :], gt[ci % 2][:, :], reads=[gt_b[ci % 2]], writes=[sb_])
            proj_fm(P, big, big_b, wv, [C_G + i * 128 for i in range(64)], evac_g, wpool, wpool_b)
            if cst is not None:
                after_phase1()
                conv_finish(sc, cst)
        P.nc.leave_named_scope("pBgate", _sg, False)
        with Scope(P) as sc:
            uT = sc.sb("uT", [128, 4, T], BF16); uT_b = Buf()
            ybT = sc.sb("ybT", [128, 4, T], BF16); ybT_b = Buf()

            def evac_u(ci, tt, ps, ps_b):
                t0, t1 = tt * 512, (tt + 1) * 512
                kb.op("act", lambda e: e.activation(out=uT[:, ci, t0:t1], in_=ps[:, :], func=AF.Gelu), reads=[ps_b], writes=[uT_b])
            proj_fm(P, big, big_b, wv, [C_U + c * 128 for c in range(4)], evac_u, wpool, wpool_b)
            wtt = sc.sb("wtt", [128, 4, 128], BF16); wtt_b = Buf()
            for g in range(4):
                kb.op("dve", lambda e: e.tensor_tensor(out=wtt[:, g, :], in0=pp[:, PP_SW + g * 128:PP_SW + (g + 1) * 128],
                                                       in1=ccf[:, CC_M0:CC_M0 + 128], op=ALU.mult),
                      reads=[P.pp_b, P.cb], writes=[wtt_b])
            bsr = sc.sb("bsr", [128, 512], BF16); bsr_b = Buf()
            kb.op("dve", lambda e: e.tensor_copy(out=bsr[:, :], in_=pp[:, PP_SBS:PP_SBS + 512]), reads=[P.pp_b], writes=[bsr_b])
            vg = [sc.sb("vg", [128, 512], F32) for _ in range(2)]; vg_b = [Buf() for _ in range(2)]
            junk = sc.sb("junk", [128, 512], F32); junk_b = Buf()
            st = [sc.sb("st", [128, 8], F32) for _ in range(2)]; st_b = [Buf() for _ in range(2)]
            vln = [sc.sb("vln", [128, 512], BF16) for _ in range(2)]; vln_b = [Buf() for _ in range(2)]

            svq = []

            def evac_sv(i, ps, ps_b):
                v, v_b = vg[i % 2], vg_b[i % 2]
                s, s_b = st[i % 2], st_b[i % 2]
                kb.op("act", lambda e: e.activation(out=v[:, :], in_=ps[:, :], func=AF.Gelu, accum_out=s[:, 0:1]),
                      reads=[ps_b], writes=[v_b, s_b])
                kb.op("act", lambda e: e.activation(out=junk[:, :], in_=v[:, :], func=AF.Square, accum_out=s[:, 1:2]),
                      reads=[v_b], writes=[junk_b, s_b])
                kb.op("dve", lambda e: e.tensor_scalar(out=s[:, 2:3], in0=s[:, 0:1], scalar1=1.0 / 512, scalar2=None, op0=ALU.mult),
                      reads=[s_b], writes=[s_b])
                kb.op("dve", lambda e: e.tensor_tensor(out=s[:, 3:4], in0=s[:, 2:3], in1=s[:, 2:3], op=ALU.mult), reads=[s_b], writes=[s_b])
                kb.op("dve", lambda e: e.scalar_tensor_tensor(out=s[:, 4:5], in0=s[:, 1:2], scalar=1.0 / 512, in1=s[:, 3:4],
                                                              op0=ALU.mult, op1=ALU.subtract), reads=[s_b], writes=[s_b])
                kb.op("act", lambda e: e.activation(out=s[:, 5:6], in_=s[:, 4:5], func=AF.Sqrt, bias=P.epsc[:, :], scale=1.0),
                      reads=[s_b, P.cb], writes=[s_b])
                kb.op("dve", lambda e: e.reciprocal(out=s[:, 6:7], in_=s[:, 5:6]), reads=[s_b], writes=[s_b])
                kb.op("dve", lambda e: e.tensor_scalar(out=v[:, :], in0=v[:, :], scalar1=s[:, 2:3], scalar2=s[:, 6:7],
                                                       op0=ALU.subtract, op1=ALU.mult), reads=[v_b, s_b], writes=[v_b])
                kb.op("pool", lambda e: e.tensor_tensor(out=v[:, :], in0=v[:, :], in1=pp[:, PP_SG:PP_SG + 512], op=ALU.mult),
                      reads=[v_b, P.pp_b], writes=[v_b])
                vl, vl_b = vln[i % 2], vln_b[i % 2]
                kb.op("pool", lambda e: e.tensor_tensor(out=vl[:, :], in0=v[:, :], in1=pp[:, PP_SB:PP_SB + 512], op=ALU.add),
                      reads=[v_b, P.pp_b], writes=[vl_b])
                if svq:
                    sv_stage2(*svq.pop(0))
                svq.append((i, vl, vl_b))

            def sv_stage2(i, vl, vl_b):
                pm, pm_b = P.next_ps()
                for g in range(4):
                    kb.op("pe", lambda e: e.matmul(pm[:, g * 128:(g + 1) * 128], lhsT=vl[:, g * 128:(g + 1) * 128], rhs=wtt[:, g, :],
                                                   start=True, stop=False), reads=[vl_b, wtt_b], writes=[pm_b], sig=False)
                    kb.op("pe", lambda e: e.matmul(pm[:, g * 128:(g + 1) * 128], lhsT=ccb[:, CC_O128:CC_O128 + 128], rhs=bsr[:, g * 128:(g + 1) * 128],
                                                   start=False, stop=True), reads=[P.cb, bsr_b], writes=[pm_b], sig=(g == 3))
                kb.op("dve", lambda e: e.tensor_tensor(out=ybT[:, :, i * 128:(i + 1) * 128], in0=uT[:, :, i * 128:(i + 1) * 128],
                                                       in1=pm[:, :].rearrange("p (g t) -> p g t", g=4), op=ALU.mult),
                      reads=[uT_b, pm_b], writes=[ybT_b])
            proj_tm(P, sc, big, big_b, wv, C_V, evac_sv)
            while svq:
                sv_stage2(*svq.pop(0))
            for g in range(4):
                kb.dma("sp", yT[512 + g * 128: 512 + (g + 1) * 128, :], ybT[:, g, :], reads=[ybT_b], writes=[sb_])

    if d.get("only_projB"):
        return projB
    if not d.get("p1_done"):
        with Scope(P) as sc0:
            big = sc0.sb("big", [128, KC * T], BF16)
            big_b = [Buf() for _ in range(4)]
            with Scope(P) as sc2:
                norm_phase(P, sc2, big, big_b, d["x1T"], PP_GM)
            wpool = [sc0.sb("wp", [128, KC, 128], BF16) for _ in range(3)]; wpool_b = [Buf() for _ in range(3)]
            projB(big, big_b, wv, wpool, wpool_b)
    if after_phase1 is not None:
        after_phase1()
    nc.leave_named_scope(_cur[0], _cur[1], False)
    _cur = ['p2conv', nc.enter_named_scope('p2conv', False)[0]]
    if not d.get("conv_done"):
        with Scope(P) as sc:
            cst = conv_main(sc)
            conv_finish(sc, cst)
    nc.leave_named_scope(_cur[0], _cur[1], False)
    _cur = ['p3dil', nc.enter_named_scope('p3dil', False)[0]]
    with Scope(P) as sc:
        kall = sc.sb("kall", [128, 4, 2 * T], BF16); kall_b = Buf()
        qc = sc.sb("qc", [128, 4, T], BF16); qc_b = Buf()
        for h in range(4):
            kb.dma("sp", kall[:, h, 0:T], d["pkCT"][h * 128:(h + 1) * 128, :], writes=[kall_b])
            kb.dma("sp", kall[:, h, T:2 * T], d["kCT"][h * 128:(h + 1) * 128, :], writes=[kall_b])
            kb.dma("sp", qc[:, h, :], qCT[h * 128:(h + 1) * 128, :], reads=[sb_], writes=[qc_b])
        num = sc.sb("num", [128, 4, T], F32); num_b = Buf()
        den = sc.sb("den", [128, 4, T], F32); den_b = Buf()
        m0r = sc.sb("m0r", [128, 4, 128], BF16); m1r = sc.sb("m1r", [128, 4, 128], BF16); mr_b = Buf()
        for g in range(4):
            kb.op("dve", lambda e: e.tensor_copy(out=m0r[:, g, :], in_=ccb[:, CC_M0:CC_M0 + 128]), reads=[P.cb], writes=[mr_b])
            kb.op("dve", lambda e: e.tensor_copy(out=m1r[:, g, :], in_=ccb[:, CC_M1:CC_M1 + 128]), reads=[P.cb], writes=[mr_b])
        vcu = [sc.sb("vcu", [128, 512], BF16) for _ in range(2)]; vcu_b = [Buf() for _ in range(2)]
        vpr = [sc.sb("vpr", [128, 512], BF16) for _ in range(2)]; vpr_b = [Buf() for _ in range(2)]
        ptc = [sc.sb("ptc", [128, 4, 128], BF16) for _ in range(2)]; ptc_b = [Buf() for _ in range(2)]
        ptp = [sc.sb("ptp", [128, 4, 128], BF16) for _ in range(2)]; ptp_b = [Buf() for _ in range(2)]
        SC_C = 128 ** -0.5
        n = 0
        dq = []

        def dil_pv(i, pi, sl_q, c_, c_b, p_, p_b):
            pso, pso_b = P.next_ps()
            for h in range(4):
                kb.op("pe", lambda e: e.matmul(pso[:, h * 128:(h + 1) * 128], lhsT=vcu[i][:, h * 128:(h + 1) * 128], rhs=c_[:, h, :],
                                               start=True, stop=False), reads=[vcu_b[i], c_b], writes=[pso_b], sig=False)
                kb.op("pe", lambda e: e.matmul(pso[:, h * 128:(h + 1) * 128], lhsT=vpr[i][:, h * 128:(h + 1) * 128], rhs=p_[:, h, :],
                                               start=False, stop=True), reads=[vpr_b[i], p_b], writes=[pso_b], sig=(h == 3))
            psd, psd_b = P.next_ps()
            kb.op("pe", lambda e: e.matmul(psd[:, :], lhsT=P.ones1[:, :], rhs=c_[:, :, :], start=True, stop=False),
                  reads=[P.cb, c_b], writes=[psd_b], sig=False)
            kb.op("pe", lambda e: e.matmul(psd[:, :], lhsT=P.ones1[:, :], rhs=p_[:, :, :], start=False, stop=True),
                  reads=[P.cb, p_b], writes=[psd_b], sig=True)
            dn = num[:, :, sl_q]
            dd = den[:, :, sl_q]
            po = pso[:, :].rearrange("p (g t) -> p g t", g=4)
            pd = psd[:, :].rearrange("p (g t) -> p g t", g=4)
            if pi == 0:
                kb.op("act", lambda e: e.activation(out=dn, in_=po, func=AF.Copy), reads=[pso_b], writes=[num_b])
                kb.op("act", lambda e: e.activation(out=dd, in_=pd, func=AF.Copy), reads=[psd_b], writes=[den_b])
            else:
                kb.op("dve", lambda e: e.tensor_tensor(out=dn, in0=dn, in1=po, op=ALU.add), reads=[pso_b, num_b], writes=[num_b])
                kb.op("dve", lambda e: e.tensor_tensor(out=dd, in0=dd, in1=pd, op=ALU.add), reads=[psd_b, den_b], writes=[den_b])

        for pi, dl in enumerate((1, 4, 16)):
            nb = T // (128 * dl)
            vown = d["vC"].rearrange("(l s) c -> l s c", s=dl)
            vpar = d["pvC"].rearrange("(l s) c -> l s c", s=dl)
            for r in range(dl):
                for blk in range(nb):
                    i = n % 2; n += 1
                    s0 = r + dl * 128 * blk
                    l0 = 128 * blk
                    sl_q = slice(s0, s0 + 127 * dl + 1, dl)
                    sl_kc = slice(T + s0, T + s0 + 127 * dl + 1, dl)
                    sl_kp = slice(T + s0 - 128 * dl, T + s0 - dl + 1, dl)
                    kb.dma("sp", vcu[i][:, :], vown[l0:l0 + 128, r, :], writes=[vcu_b[i]])
                    if blk == 0:
                        kb.dma("sp", vpr[i][:, :], vpar[T // dl - 128:T // dl, r, :], writes=[vpr_b[i]])
                    else:
                        kb.dma("sp", vpr[i][:, :], vown[l0 - 128:l0, r, :], writes=[vpr_b[i]])
                    psc, psc_b = P.next_ps()
                    psp, psp_b = P.next_ps()
                    for h in range(4):
                        kb.op("pe", lambda e: e.matmul(psc[:, h * 128:(h + 1) * 128], lhsT=kall[:, h, sl_kc], rhs=qc[:, h, sl_q],
                                                       start=True, stop=True), reads=[kall_b, qc_b], writes=[psc_b], sig=(h == 3))
                    for h in range(4):
                        kb.op("pe", lambda e: e.matmul(psp[:, h * 128:(h + 1) * 128], lhsT=kall[:, h, sl_kp], rhs=qc[:, h, sl_q],
                                                       start=True, stop=True), reads=[kall_b, qc_b], writes=[psp_b], sig=(h == 3))
                    c_, c_b = ptc[i], ptc_b[i]
                    p_, p_b = ptp[i], ptp_b[i]
                    kb.op("act", lambda e: e.activation(out=c_[:, :, :], in_=psc[:, :].rearrange("p (g t) -> p g t", g=4), func=AF.Exp, scale=SC_C),
                          reads=[psc_b], writes=[c_b])
                    if blk == 0:
                        kb.op("act", lambda e: e.activation(out=p_[:, :, :], in_=psp[:, :].rearrange("p (g t) -> p g t", g=4), func=AF.Exp,
                                                            scale=SC_C, bias=flagb), reads=[psp_b, P.cb], writes=[p_b])
                    else:
                        kb.op("act", lambda e: e.activation(out=p_[:, :, :], in_=psp[:, :].rearrange("p (g t) -> p g t", g=4), func=AF.Exp,
                                                            scale=SC_C), reads=[psp_b], writes=[p_b])
                    kb.op("dve", lambda e: e.tensor_tensor(out=c_[:, :, :], in0=c_[:, :, :], in1=m0r[:, :, :], op=ALU.mult),
                          reads=[c_b, mr_b], writes=[c_b])
                    kb.op("pool", lambda e: e.tensor_tensor(out=p_[:, :, :], in0=p_[:, :, :], in1=m1r[:, :, :], op=ALU.mult),
                          reads=[p_b, mr_b], writes=[p_b])
                    dq.append((i, pi, sl_q, c_, c_b, p_, p_b))
                    if len(dq) > 1:
                        dil_pv(*dq.pop(0))
        while dq:
            dil_pv(*dq.pop(0))
        yct = sc.sb("yct", [128, 4, T], BF16); yct_b = Buf()
        for h in range(4):
            kb.op("dve", lambda e: e.reciprocal(out=den[:, h, :], in_=den[:, h, :]), reads=[den_b], writes=[den_b])
            kb.op("dve", lambda e: e.tensor_tensor(out=yct[:, h, :], in0=num[:, h, :], in1=den[:, h, :], op=ALU.mult),
                  reads=[num_b, den_b], writes=[yct_b])
            kb.dma("sp", yT[1024 + h * 128:1024 + (h + 1) * 128, :], yct[:, h, :], reads=[yct_b], writes=[sb_])
    nc.leave_named_scope(_cur[0], _cur[1], False)
    _cur = ['p4diff', nc.enter_named_scope('p4diff', False)[0]]
    with Scope(P) as sc:
        kall = sc.sb("kalld", [128, 4, 2 * T], BF16); kall_b = Buf()
        qdp = [sc.sb("qdp", [128, 4, T], BF16) for _ in range(2)]; qd_b = Buf()
        vall = sc.sb("vall", [128, 32, 512], BF16); vall_b = Buf()
        kb.op("pool", lambda e: e.memset(qdp[0][64:128, :, :], 0.0), writes=[qd_b])
        kb.op("pool", lambda e: e.memset(qdp[1][0:64, :, :], 0.0), writes=[qd_b])
        for h in range(4):
            kb.dma("sp", kall[:, h, 0:T], d["pkDT"][h * 128:(h + 1) * 128, :], writes=[kall_b])
            kb.dma("sp", kall[:, h, T:2 * T], d["kDT"][h * 128:(h + 1) * 128, :], writes=[kall_b])
            kb.dma("sp", qdp[0][0:64, h, :], qDT[h * 128:h * 128 + 64, :], reads=[sb_], writes=[qd_b])
            kb.dma("sp", qdp[1][64:128, h, :], qDT[h * 128 + 64:(h + 1) * 128, :], reads=[sb_], writes=[qd_b])
        for b4 in range(0, 16, 4):
            kb.dma("sp", vall[:, b4:b4 + 4, :], d["pvD"].rearrange("(b p) c -> p b c", p=128)[:, b4:b4 + 4, :], writes=[vall_b])
            kb.dma("sp", vall[:, 16 + b4:16 + b4 + 4, :], d["vD"].rearrange("(b p) c -> p b c", p=128)[:, b4:b4 + 4, :], writes=[vall_b])
        lt = sc.sb("lt", [128, 64], F32); ls = sc.sb("ls", [128, 8], F32); ls_b = Buf()
        for j in range(2):
            kb.op("dve", lambda e: e.tensor_tensor(out=lt[:, :], in0=pp[:, PP_LAM + j * 128:PP_LAM + j * 128 + 64],
                                                   in1=pp[:, PP_LAM + j * 128 + 64:PP_LAM + j * 128 + 128], op=ALU.mult),
                  reads=[P.pp_b, ls_b], writes=[ls_b])
            kb.op("dve", lambda e: e.reduce_sum(out=ls[:, j:j + 1], in_=lt[:, :], axis=AX.X), reads=[ls_b], writes=[ls_b])
            kb.op("act", lambda e: e.activation(out=ls[:, 2 + j:3 + j], in_=ls[:, j:j + 1], func=AF.Exp), reads=[ls_b], writes=[ls_b])
        kb.op("dve", lambda e: e.tensor_tensor(out=ls[:, 4:5], in0=ls[:, 3:4], in1=ls[:, 2:3], op=ALU.subtract), reads=[ls_b], writes=[ls_b])
        kb.op("dve", lambda e: e.tensor_tensor(out=ls[:, 5:6], in0=ls[:, 4:5], in1=pp[:, PP_LI:PP_LI + 1], op=ALU.subtract),
              reads=[ls_b, P.pp_b], writes=[ls_b])
        kb.op("dve", lambda e: e.tensor_tensor(out=ls[:, 6:7], in0=pp[:, PP_SUB:PP_SUB + 1], in1=pp[:, PP_OML:PP_OML + 1], op=ALU.mult),
              reads=[ls_b, P.pp_b], writes=[ls_b])
        neglam = ls[:, 5:6]
        subs = ls[:, 6:7]
        pt = [sc.sb("ptd", [128, 512], BF16) for _ in range(6)]; pt_b = [Buf() for _ in range(6)]
        o0 = sc.sb("o0", [128, 512], F32); o1 = sc.sb("o1", [128, 512], F32); rr = sc.sb("rr", [128, 512], F32); o_b = Buf()
        osq = sc.sb("osq", [128, 512], BF16); osq_b = Buf()
        ydt = [sc.sb("ydt", [128, 512], BF16) for _ in range(2)]; ydt_b = [Buf() for _ in range(2)]
        n = 0
        si = 0
        LA = 3
        for h in range(4):
            for qt in range(4):
                q0, q1 = qt * 512, (qt + 1) * 512
                nkb = 16 + 4 * qt + 4
                accs = [(P.psum[4 + a], P.psum_b[4 + a]) for a in range(4)]
                steps = [(kbi, c) for kbi in range(nkb) for c in range(2)]
                pend = []

                def emit_pv(item):
                    kbi, c, p_, p_b = item
                    last = (kbi == nkb - 1)
                    kb.op("pe", lambda e: e.matmul(accs[c][0][:, :], lhsT=vall[:, kbi, h * 128:(h + 1) * 128], rhs=p_[:, :],
                                                   start=(kbi == 0), stop=last), reads=[vall_b, p_b], writes=[accs[c][1]], sig=last)
                    kb.op("pe", lambda e: e.matmul(accs[2 + c][0][:, :], lhsT=P.ones1[:, :], rhs=p_[:, :],
                                                   start=(kbi == 0), stop=last), reads=[P.cb, p_b], writes=[accs[2 + c][1]], sig=last)
                for (kbi, c) in steps:
                    ps, ps_b = P.psum[si % 4], P.psum_b[si % 4]; si += 1
                    kb.op("pe", lambda e: e.matmul(ps[:, :], lhsT=kall[:, h, kbi * 128:(kbi + 1) * 128],
                                                   rhs=qdp[c][:, h, q0:q1], start=True, stop=True),
                          reads=[kall_b, qd_b], writes=[ps_b])
                    p_, p_b = pt[n % 6], pt_b[n % 6]; n += 1
                    if kbi < 16:
                        kb.op("act", lambda e: e.activation(out=p_[:, :], in_=ps[:, :], func=AF.Exp, scale=0.125, bias=flagb),
                              reads=[ps_b, P.cb], writes=[p_b])
                    else:
                        kb.op("act", lambda e: e.activation(out=p_[:, :], in_=ps[:, :], func=AF.Exp, scale=0.125), reads=[ps_b], writes=[p_b])
                    jr = kbi - 16 - 4 * qt
                    if jr >= 0:
                        eng = "dve" if c == 0 else "pool"
                        kb.op(eng, lambda e: e.tensor_tensor(out=p_[:, :], in0=p_[:, :], in1=ccb[:, CC_DM + jr * 512:CC_DM + (jr + 1) * 512],
                                                             op=ALU.mult), reads=[p_b, P.cb], writes=[p_b])
                    pend.append((kbi, c, p_, p_b))
                    if len(pend) > LA:
                        emit_pv(pend.pop(0))
                while pend:
                    emit_pv(pend.pop(0))
                kb.op("dve", lambda e: e.reciprocal(out=rr[:, :], in_=accs[2][0][:, :]), reads=[accs[2][1], o_b], writes=[o_b])
                kb.op("dve", lambda e: e.tensor_tensor(out=o0[:, :], in0=accs[0][0][:, :], in1=rr[:, :], op=ALU.mult), reads=[accs[0][1], o_b], writes=[o_b])
                kb.op("dve", lambda e: e.reciprocal(out=rr[:, :], in_=accs[3][0][:, :]), reads=[accs[3][1], o_b], writes=[o_b])
                kb.op("dve", lambda e: e.tensor_tensor(out=o1[:, :], in0=accs[1][0][:, :], in1=rr[:, :], op=ALU.mult), reads=[accs[1][1], o_b], writes=[o_b])
                kb.op("dve", lambda e: e.scalar_tensor_tensor(out=o0[:, :], in0=o1[:, :], scalar=neglam, in1=o0[:, :], op0=ALU.mult, op1=ALU.add),
                      reads=[o_b, ls_b], writes=[o_b])
                kb.op("act", lambda e: e.activation(out=osq[:, :], in_=o0[:, :], func=AF.Square), reads=[o_b], writes=[osq_b])
                ps, ps_b = P.psum[si % 4], P.psum_b[si % 4]; si += 1
                kb.op("pe", lambda e: e.matmul(ps[:, :], lhsT=ccb[:, CC_O128:CC_O128 + 128], rhs=osq[:, :], start=True, stop=True),
                      reads=[P.cb, osq_b], writes=[ps_b])
                kb.op("act", lambda e: e.activation(out=rr[:, :], in_=ps[:, :], func=AF.Sqrt, bias=P.epsc[:, :], scale=1.0),
                      reads=[ps_b, P.cb, o_b], writes=[o_b])
                kb.op("dve", lambda e: e.reciprocal(out=rr[:, :], in_=rr[:, :]), reads=[o_b], writes=[o_b])
                y_, y_b = ydt[qt % 2], ydt_b[qt % 2]
                kb.op("dve", lambda e: e.scalar_tensor_tensor(out=y_[:, :], in0=o0[:, :], scalar=subs, in1=rr[:, :], op0=ALU.mult, op1=ALU.mult),
                      reads=[o_b, ls_b], writes=[y_b])
                kb.dma("sp", yT[1536 + h * 128:1536 + (h + 1) * 128, q0:q1], y_[:, :], reads=[y_b], writes=[sb_])
    nc.leave_named_scope(_cur[0], _cur[1], False)
    _cur = ['p5merge', nc.enter_named_scope('p5merge', False)[0]]
    out2_b = Buf("x2")
    with Scope(P) as sc:
        mT = sc.sb("mT", [128, KC * T], BF16); mT_b = [Buf() for _ in range(4)]
        with Scope(P) as sc1:
            yall = sc1.sb("yall", [128, KC, 1024], BF16); yall_b = Buf()
            yv = yT.rearrange("(k p) t -> p k t", p=128)
            wbv = d["w_br"].rearrange("(k p) c -> p k c", p=128)
            wp = [sc1.sb("wpb", [128, KC, 128], BF16) for _ in range(2)]; wp_b = [Buf() for _ in range(2)]
            gtl = [sc1.sb("gtl", [128, 4, 1024], BF16) for _ in range(2)]; gtl_b = [Buf() for _ in range(2)]
            ac = [sc1.sb("mac", [128, 512], F32) for _ in range(2)]; ac_b = [Buf() for _ in range(2)]
            tm = [sc1.sb("mtm", [128, 512], F32) for _ in range(2)]; tm_b = [Buf() for _ in range(2)]
            gv = gT.rearrange("(b r) t -> r b t", b=4)
            n = 0
            for th in range(2):
                for k4 in range(0, KC, 4):
                    kb.dma("sp", yall[:, k4:k4 + 4, :], yv[:, k4:k4 + 4, th * 1024:(th + 1) * 1024], reads=[sb_], writes=[yall_b])
                for m in range(KC):
                    w, w_b = wp[m % 2], wp_b[m % 2]
                    kb.dma("pool", w[:, :, :], wbv[:, :, m * 128:(m + 1) * 128], writes=[w_b])
                    gl, gl_b = gtl[m % 2], gtl_b[m % 2]
                    kb.dma("sp", gl[:, :, :], gv[m * 128:(m + 1) * 128, :, th * 1024:(th + 1) * 1024], reads=[sb_], writes=[gl_b])
                    for t2 in range(2):
                        tt = th * 2 + t2
                        t0, t1 = tt * 512, (tt + 1) * 512
                        l0, l1 = t2 * 512, (t2 + 1) * 512
                        g_, g_b = gl[:, :, l0:l1], gl_b
                        a_, a_b = ac[n % 2], ac_b[n % 2]
                        n += 1
                        for br in range(4):
                            ps, ps_b = P.next_ps()
                            for kc in range(4):
                                kb.op("pe", lambda e: e.matmul(ps[:, :], lhsT=w[:, br * 4 + kc, :], rhs=yall[:, br * 4 + kc, l0:l1],
                                                               start=(kc == 0), stop=(kc == 3)), reads=[w_b, yall_b], writes=[ps_b], sig=(kc == 3))
                            if br == 0:
                                kb.op("dve", lambda e: e.tensor_tensor(out=a_[:, :], in0=ps[:, :], in1=g_[:, 0, :], op=ALU.mult),
                                      reads=[ps_b, g_b], writes=[a_b])
                            else:
                                t_, t_b = tm[br % 2], tm_b[br % 2]
                                kb.op("dve", lambda e: e.tensor_tensor(out=t_[:, :], in0=ps[:, :], in1=g_[:, br, :], op=ALU.mult),
                                      reads=[ps_b, g_b], writes=[t_b])
                                if br < 3:
                                    kb.op("dve", lambda e: e.tensor_tensor(out=a_[:, :], in0=a_[:, :], in1=t_[:, :], op=ALU.add),
                                          reads=[a_b, t_b], writes=[a_b])
                                else:
                                    kb.op("dve", lambda e: e.tensor_tensor(out=hT_view(mT, m, t0, t1), in0=a_[:, :], in1=t_[:, :], op=ALU.add),
                                          reads=[a_b, t_b], writes=[mT_b[tt]])
        wov = d["w_out"].rearrange("(k p) c -> p k c", p=128)
        wp = [sc.sb("wpo", [128, KC, 128], BF16) for _ in range(2)]; wp_b = [Buf() for _ in range(2)]
        xr = [sc.sb("xro", [128, 1024], F32) for _ in range(2)]; xr_b = [Buf() for _ in range(2)]
        n = 0
        for m in range(KC):
            w, w_b = wp[m % 2], wp_b[m % 2]
            kb.dma("pool", w[:, :, :], wov[:, :, m * 128:(m + 1) * 128], writes=[w_b])
            for th in range(2):
                h0 = th * 1024
                x_, x_b = xr[n % 2], xr_b[n % 2]; n += 1
                kb.dma("sp", x_[:, :], d["x1T"][m * 128:(m + 1) * 128, h0:h0 + 1024], writes=[x_b])
                for hh in range(2):
                    t0 = h0 + hh * 512
                    ps, ps_b = P.next_ps()
                    for k in range(KC):
                        kb.op("pe", lambda e: e.matmul(ps[:, :], lhsT=w[:, k, :], rhs=hT_view(mT, k, t0, t0 + 512),
                                                       start=(k == 0), stop=(k == KC - 1)), reads=[w_b, mT_b[t0 // 512]], writes=[ps_b], sig=(k == KC - 1))
                    kb.op("dve", lambda e: e.tensor_tensor(out=x_[:, hh * 512:(hh + 1) * 512], in0=ps[:, :], in1=x_[:, hh * 512:(hh + 1) * 512], op=ALU.add),
                          reads=[ps_b, x_b], writes=[x_b])
                kb.dma("sp", x2T[m * 128:(m + 1) * 128, h0:h0 + 1024], x_[:, :], reads=[x_b], writes=[out2_b])
    nc.leave_named_scope(_cur[0], _cur[1], False)
    kb.barrier()
    out_b = Buf("x3")
    ffn_phase(P, x2T, d["x3T"], PP_G2, d["w13b"], d["w2b"], d["actT"], out_b)
    return [out_b]


import math
import ml_dtypes
THETA = 500000.0
_BF = ml_dtypes.bfloat16
NLAYER = 4
WNAMES = ("w13a", "w2a", "w_in", "w_br", "w_out", "w13b", "w2b")
WSHAPES = {"w13a": [D, 2 * DFF], "w2a": [DFF, D], "w_in": [D, INC], "w_br": [D, D], "w_out": [D, D],
           "w13b": [D, 2 * DFF], "w2b": [DFF, D]}


def build_fused(nlayer=NLAYER, groups=None):
    groups = groups or [[0, 1], [2, 3], [4, 5], [6, 7]]
    nc = bass.Bass("TRN2", target_bir_lowering=False)
    di = lambda n, s, t: nc.dram_tensor(n, s, t, kind="ExternalInput").ap()
    do = lambda n, s, t: nc.dram_tensor(n, s, t, kind="ExternalOutput").ap()
    xT = di("xT", [D, T], F32)
    outT = do("outT", [D, T], F32)
    cc = di("cc", [128, NCC], F32)
    pos = di("pos", [128, T], I32)
    pps = [di("pp%d" % l, [128, NPP], F32) for l in range(nlayer)]
    W = [{n: di("%s_%d" % (n, l), WSHAPES[n], F32) for n in WNAMES} for l in range(nlayer)]
    with ExitStack() as es:
        P = Prog(nc, es, pps[0], cc, pos)
        kb = P.kb
        dr = kb.dram
        S = dict(actT=dr("actT", [DFF, T], BF16), x1T=dr("x1T", [D, T], F32), zT=dr("zT", [512, T], F32),
                 qCT=dr("qCT", [512, T], BF16), qDT=dr("qDT", [512, T], BF16), gT=dr("gT", [8192, T], BF16),
                 yT=dr("yT", [2048, T], BF16), x2T=dr("x2T", [D, T], F32))
        xs = [dr("xs0", [D, T], F32), dr("xs1", [D, T], F32)]
        EX = [dr("EX%d" % i, [T, 512], BF16) for i in range(4)]
        EXZ = dr("EXZ", [512, 32], F32)
        GA = [dr("GA%d" % i, [2 * T, 512], BF16) for i in range(4)]
        GAZ = dr("GAZ", [1024, 32], F32)
        fm = lambda ap: ap.rearrange("(f a) c -> f (a c)", a=4)
        outs = []
        for l in range(nlayer):
            if l > 0:
                kb.barrier()
                kb.new_epoch()
                kb.dma("sp", P.pp[:, :], pps[l], writes=[P.pp_b])
            x_in = xT if l == 0 else xs[(l - 1) % 2]
            x_out = outT if l == nlayer - 1 else xs[l % 2]
            dA = dict(xT=x_in, w13a=W[l]["w13a"], w2a=W[l]["w2a"], w_in=W[l]["w_in"], x1T=S["x1T"], zT=S["zT"],
                      kCT=fm(EX[0]), kDT=fm(EX[1]), vC=EX[2], vD=EX[3],
                      halo=EXZ, actT=S["actT"])
            ga_keys = []

            def issue_cc(l=l, ga_keys=ga_keys):
                for sk, v in kb.ring_val.items():
                    if v > 0 and sk[1] == "sp":
                        kb._need("pool", (sk, v))
                for j, (src, dst) in enumerate([(EX[i], GA[i]) for i in range(4)] + [(EXZ, GAZ)]):
                    key = "cc_%d_%d" % (l, j)
                    kb.sems[key] = es.enter_context(nc.semaphore(key))
                    ins = nc.gpsimd.collective_compute("AllGather", ALU.bypass, replica_groups=groups,
                                                       ins=[src.opt()], outs=[dst.opt()])
                    ins.then_inc(kb.sems[key])
                    kb.n_inst += 1
                    ga_keys.append(key)

            def wait_cc(ga_keys=ga_keys):
                for e in ("pe", "act", "dve", "pool", "sp"):
                    for key in ga_keys:
                        kb._need(e, (key, 1))
            dB = dict(x1T=S["x1T"], w13b=W[l]["w13b"], w2b=W[l]["w2b"], w_in=W[l]["w_in"], w_br=W[l]["w_br"], w_out=W[l]["w_out"],
                      zT=S["zT"], kCT=dA["kCT"], kDT=dA["kDT"], vC=dA["vC"], vD=dA["vD"],
                      phalo=GAZ[0:512], pkCT=fm(GA[0][0:T]), pkDT=fm(GA[1][0:T]), pvC=GA[2][0:T], pvD=GA[3][0:T],
                      x3T=x_out, actT=S["actT"], qCT=S["qCT"], qDT=S["qDT"], gT=S["gT"], yT=S["yT"], x2T=S["x2T"])
            dB["conv_in_proj"] = True
            dA["projB"] = launchB_body(P, dict(dB, only_projB=True), after_phase1=wait_cc)
            dA["after_exports"] = issue_cc
            launchA_body(P, dA)
            dB["p1_done"] = True
            dB["conv_done"] = True
            outs = launchB_body(P, dB, after_phase1=wait_cc)
        kb.finish(outs)
        print("fused n_inst", kb.n_inst)
    return nc


def kernel(**inp):
    NCORE = 8
    x = np.asarray(inp["x"], np.float32)
    positions = np.asarray(inp["positions"], np.int32)
    nc = build_fused()
    common = {}
    for l in range(NLAYER):
        common["pp%d" % l] = make_pp(inp, l)
        common["w13a_%d" % l] = np.asarray(inp["ffn1_w13"][l], np.float32)
        common["w2a_%d" % l] = np.asarray(inp["ffn1_w2"][l], np.float32)
        common["w_in_%d" % l] = np.asarray(inp["w_in"][l], np.float32)
        common["w_br_%d" % l] = np.ascontiguousarray(np.asarray(inp["w_branch"][l], np.float32).reshape(D, D))
        common["w_out_%d" % l] = np.asarray(inp["w_out"][l], np.float32)
        common["w13b_%d" % l] = np.asarray(inp["ffn2_w13"][l], np.float32)
        common["w2b_%d" % l] = np.asarray(inp["ffn2_w2"][l], np.float32)
    in_maps = []
    for c in range(NCORE):
        b, half = c // 2, c % 2
        sl = slice(half * T, (half + 1) * T)
        m = dict(common)
        m["cc"] = make_consts(half == 1)
        m["pos"] = np.ascontiguousarray(np.broadcast_to(positions[b, sl][None, :], (128, T))).astype(np.int32)
        m["xT"] = np.ascontiguousarray(x[b, sl].T)
        in_maps.append(m)
    res = run_bass_kernel_spmd(nc, in_maps, core_ids=list(range(NCORE))).results
    out = np.zeros((4, 2 * T, D), np.float32)
    for c in range(NCORE):
        b, half = c // 2, c % 2
        out[b, half * T:(half + 1) * T] = np.asarray(res[c]["outT"], np.float32).T
    return out


def make_consts(has_prev):
    cc = np.zeros((128, NCC), np.float32)
    p = np.arange(128)
    invc = np.where(p < 32, THETA ** (-(p % 16) * 2.0 / 32), 0.0)
    pd = p % 64
    invd = np.where(pd < 16, THETA ** (-(pd % 8) * 2.0 / 16), 0.0)
    cc[:, CC_INVF] = invc; cc[:, CC_INVF + 1] = invd
    cc[:, CC_FLAG] = 1.0 if has_prev else 0.0
    cc[:, CC_FLAG + 1] = 0.0 if has_prev else NEG
    Rc = np.zeros((128, 128), np.float32)
    for m in range(16):
        Rc[m + 16, m] = -1.0; Rc[m, m + 16] = 1.0
    Rd = np.zeros((128, 128), np.float32)
    for b in (0, 64):
        for m in range(8):
            Rd[b + m + 8, b + m] = -1.0; Rd[b + m, b + m + 8] = 1.0
    cc[:, CC_RC:CC_RC + 128] = Rc; cc[:, CC_RD:CC_RD + 128] = Rd
    cc[:, CC_O128:CC_O128 + 128] = 1.0 / 128
    o64 = np.zeros((128, 128), np.float32); o64[:64, :64] = 1.0 / 64; o64[64:, 64:] = 1.0 / 64
    cc[:, CC_O64:CC_O64 + 128] = o64
    k = np.arange(128)[:, None]; q = np.arange(128)[None, :]
    cc[:, CC_M0:CC_M0 + 128] = (k <= q); cc[:, CC_M1:CC_M1 + 128] = (q <= k)
    for jr in range(4):
        for jb in range(4):
            blk = np.zeros((128, 128), np.float32) if jb < jr else ((k <= q).astype(np.float32) if jb == jr else np.ones((128, 128), np.float32))
            cc[:, CC_DM + jr * 512 + jb * 128: CC_DM + jr * 512 + (jb + 1) * 128] = blk
    return cc

def make_pp(inp, l):
    pp = np.zeros((128, NPP), np.float32)
    f = lambda k: np.asarray(inp[k][l], np.float32)
    pp[:, PP_G1:PP_G1 + 16] = f('ffn1_norm').reshape(16, 128).T
    pp[:, PP_GM:PP_GM + 16] = f('mix_norm').reshape(16, 128).T
    pp[:, PP_G2:PP_G2 + 16] = f('ffn2_norm').reshape(16, 128).T
    pp[:, PP_QKG + 0] = f('dil_q_norm'); pp[:, PP_QKG + 1] = f('dil_k_norm')
    pp[:, PP_QKG + 2] = np.tile(f('diff_q_norm'), 2); pp[:, PP_QKG + 3] = np.tile(f('diff_k_norm'), 2)
    cw = f('conv_w')
    pp[:, PP_CW:PP_CW + 124] = cw.T.reshape(4, 128, 31).transpose(1, 0, 2).reshape(128, 124)
    pp[:, PP_CB:PP_CB + 4] = f('conv_b').reshape(4, 128).T
    pp[:, PP_CG:PP_CG + 4] = f('conv_ln_g').reshape(4, 128).T
    pp[:, PP_CBE:PP_CBE + 4] = f('conv_ln_b').reshape(4, 128).T
    pp[:, PP_SUB] = f('diff_subln')
    li = 0.8 - 0.6 * math.exp(-0.3 * l)
    pp[:, PP_LI] = li; pp[:, PP_OML] = 1.0 - li
    pp[:, PP_LAM:PP_LAM + 256] = np.concatenate([f('diff_lq1'), f('diff_lk1'), f('diff_lq2'), f('diff_lk2')])[None, :]
    pp[:, PP_SG:PP_SG + 512] = f('sgu_ln_g')[None, :]
    pp[:, PP_SB:PP_SB + 512] = f('sgu_ln_b')[None, :]
    sw = f('sgu_w')
    pp[:, PP_SW:PP_SW + 512] = sw.transpose(2, 0, 1).reshape(128, 512)
    pp[:, PP_SBS:PP_SBS + 512] = f('sgu_b').reshape(1, 512)
    return pp
```

```python
from contextlib import ExitStack
import numpy as np
import concourse.bass as bass
import concourse.mybir as mybir
from concourse.bass_utils import run_bass_kernel_spmd

F32 = mybir.dt.float32
BF16 = mybir.dt.bfloat16
I32 = mybir.dt.int32
AF = mybir.ActivationFunctionType
ALU = mybir.AluOpType
AX = mybir.AxisListType


class Buf:
    __slots__ = ("name", "w", "r")

    def __init__(self, name=""):
        self.name = name
        self.w = None
        self.r = {}


class KB:
    RINGN = {"sp": 8, "pool": 4}

    def __init__(self, nc, es: ExitStack):
        self.nc = nc
        self.es = es
        self.eng = {"pe": nc.tensor, "act": nc.scalar, "dve": nc.vector,
                    "pool": nc.gpsimd, "sp": nc.sync}
        self.sems = {}
        self.cnt = {}
        self.k = {}
        self.seen = {e: {} for e in self.eng}
        self.ring_i = {}
        self.ring_val = {}
        self.epoch = -1
        self.n_inst = 0
        self.new_epoch()

    def new_epoch(self):
        self.epoch += 1
        ep = self.epoch
        for e in ("pe", "act", "dve", "pool"):
            self.k[e] = (e, ep)
            self.sems[self.k[e]] = self.es.enter_context(self.nc.semaphore("s_%s_%d" % (e, ep)))
            self.cnt[e] = 0
        self.ring_val = {}
        for q, n in self.RINGN.items():
            self.ring_i[q] = 0
            for s in range(n):
                k = ("ring", q, s, ep)
                self.sems[k] = self.es.enter_context(self.nc.semaphore("r_%s%d_%d" % (q, s, ep)))
                self.ring_val[k] = 0

    def sb(self, name, shape, dtype):
        return self.es.enter_context(self.nc.sbuf_tensor(name, list(shape), dtype))

    def ps(self, name, shape=(128, 512), dtype=F32):
        return self.es.enter_context(self.nc.psum_tensor(name, list(shape), dtype))

    def dram(self, name, shape, dtype, kind="Internal"):
        return self.nc.dram_tensor(name, list(shape), dtype, kind=kind).ap()

    def _need(self, e, tok):
        sk, val = tok
        if self.seen[e].get(sk, 0) >= val:
            return
        self.eng[e].wait_ge(self.sems[sk], val)
        self.n_inst += 1
        self.seen[e][sk] = val

    def _sync(self, e, reads, writes):
        me = self.k.get(e)
        for b in reads:
            if b.w is not None:
                self._need(e, b.w)
        for b in writes:
            if b.w is not None and b.w[0] != me:
                self._need(e, b.w)
            for sk, val in b.r.items():
                if sk != me or e != "pe":
                    self._need(e, (sk, val))

    def _mark(self, tok, reads, writes):
        sk, val = tok
        for b in reads:
            b.r[sk] = val
        for b in writes:
            b.w = tok
            b.r = {}

    def op(self, e, fn, reads=(), writes=(), sig=True):
        self._sync(e, reads, writes)
        ins = fn(self.eng[e])
        self.n_inst += 1
        if sig:
            self.cnt[e] += 1
            ins.then_inc(self.sems[self.k[e]], 1)
            tok = (self.k[e], self.cnt[e])
        else:
            tok = (self.k[e], self.cnt[e] + 1)
        self._mark(tok, reads, writes)
        return ins

    def dma(self, q, out, in_, reads=(), writes=(), **kw):
        i = self.ring_i[q]
        self.ring_i[q] = (i + 1) % self.RINGN[q]
        sk = ("ring", q, i, self.epoch)
        if self.ring_val[sk] > 0:
            self._need(q, (sk, self.ring_val[sk]))
        self._sync(q, reads, writes)
        ins = self.eng[q].dma_start(out=out, in_=in_, **kw)
        self.n_inst += 1
        self.ring_val[sk] += 16
        ins.then_inc(self.sems[sk], 16)
        self._mark((sk, self.ring_val[sk]), reads, writes)
        return ins

    def barrier(self):
        for e in ("pe", "act", "dve", "pool", "sp"):
            for k in ("pe", "act", "dve", "pool"):
                if k != e and self.cnt[k] > 0:
                    self._need(e, (self.k[k], self.cnt[k]))
            for sk, v in self.ring_val.items():
                if v > 0:
                    self._need(e, (sk, v))

    def finish(self, bufs, e="sp"):
        for b in bufs:
            if b.w is not None:
                self._need(e, b.w)
        for k in ("pe", "act", "dve", "pool"):
            if self.cnt[k] > 0:
                self._need(e, (self.k[k], self.cnt[k]))
        for sk, v in self.ring_val.items():
            if v > 0:
                self._need(e, (sk, v))


D = 2048
T = 2048
DFF = 5632
KC = D // 128
NJ = DFF // 128
EPS = 1e-6

C_A, C_AG, C_U, C_V = 0, 512, 1024, 1536
C_CQ, C_CK, C_CV = 2048, 2560, 3072
C_DQ, C_DK, C_DV = 3584, 4096, 4608
C_G = 5120
INC = 13312
PP_G1, PP_GM, PP_G2, PP_QKG = 0, 16, 32, 48
PP_CW, PP_CB, PP_CG, PP_CBE = 52, 176, 180, 184
PP_SUB, PP_LI, PP_OML, PP_LAM = 188, 189, 190, 191
PP_SG, PP_SB, PP_SW, PP_SBS = 447, 959, 1471, 1983
NPP = 2495
CC_INVF, CC_FLAG = 0, 2
CC_RC, CC_RD, CC_O128, CC_O64, CC_M0, CC_M1 = 4, 132, 260, 388, 516, 644
CC_DM = 772
NCC = 772 + 2048
NEG = -30000.0


class Scope:
    cnt = 0

    def __init__(self, P):
        self.P = P

    def __enter__(self):
        self.es = ExitStack()
        self.es.__enter__()
        return self

    def sb(self, name, shape, dtype):
        Scope.cnt += 1
        return self.es.enter_context(self.P.nc.sbuf_tensor("%s_%d" % (name, Scope.cnt), list(shape), dtype))

    def __exit__(self, *a):
        self.P.kb.barrier()
        return self.es.__exit__(*a)


class Prog:
    def __init__(self, nc, es, pp_dram, cc_dram, pos_dram):
        self.nc = nc
        self.kb = KB(nc, es)
        kb = self.kb
        self.psum = [kb.ps("ps%d" % i) for i in range(8)]
        self.psum_b = [Buf("ps%d" % i) for i in range(8)]
        self.ps_i = 0
        self.pp = kb.sb("pp_sb", [128, NPP], F32)
        self.pp_b = Buf("pp")
        kb.dma("sp", self.pp[:, :], pp_dram, writes=[self.pp_b])
        self.cb = Buf("consts")
        ccf = kb.sb("ccf", [128, NCC], F32)
        self.ccf = ccf
        kb.dma("sp", ccf[:, :], cc_dram, writes=[self.cb])
        self.ccb = kb.sb("ccb", [128, NCC], BF16)
        kb.op("dve", lambda e: e.tensor_copy(out=self.ccb[:, :], in_=ccf[:, :]), reads=[self.cb], writes=[self.cb])
        self.onesD = kb.sb("onesD", [128, 128], BF16)
        kb.op("dve", lambda e: e.memset(self.onesD[:, :], 1.0 / D), writes=[self.cb])
        self.ones1 = kb.sb("ones1", [128, 128], BF16)
        kb.op("dve", lambda e: e.memset(self.ones1[:, :], 1.0), writes=[self.cb])
        self.onesLN = kb.sb("onesLN", [128, 128], F32)
        kb.op("dve", lambda e: e.memset(self.onesLN[:, :], 1.0 / 512), writes=[self.cb])
        self.epsc = kb.sb("epsc", [128, 1], F32)
        kb.op("dve", lambda e: e.memset(self.epsc[:, :], EPS), writes=[self.cb])
        self.cs = kb.sb("cs", [128, 4, T], F32)
        with Scope(self) as sc:
            posi = sc.sb("posi", [128, T], I32)
            posf = sc.sb("posf", [128, T], F32)
            ang = sc.sb("ang", [128, T], F32)
            kf = sc.sb("kf", [128, T], F32)
            ki = sc.sb("ki", [128, T], I32)
            b = Buf("cs_tmp")
            TWO_PI = float(2 * np.pi)
            kb.dma("sp", posi[:, :], pos_dram, writes=[b])
            kb.op("dve", lambda e: e.tensor_copy(out=posf[:, :], in_=posi[:, :]), reads=[b], writes=[b])
            for ty in range(2):
                for cs_i, shift in ((0, 0.5 * np.pi), (1, 0.0)):
                    dst = self.cs[:, 2 * ty + cs_i, :]
                    kb.op("dve", lambda e: e.tensor_scalar(out=ang[:, :], in0=posf[:, :], scalar1=ccf[:, CC_INVF + ty:CC_INVF + ty + 1],
                                                           scalar2=float(shift), op0=ALU.mult, op1=ALU.add),
                          reads=[b, self.cb], writes=[b])
                    kb.op("dve", lambda e: e.tensor_scalar(out=kf[:, :], in0=ang[:, :], scalar1=1.0 / TWO_PI, scalar2=None, op0=ALU.mult),
                          reads=[b], writes=[b])
                    kb.op("dve", lambda e: e.tensor_copy(out=ki[:, :], in_=kf[:, :]), reads=[b], writes=[b])
                    kb.op("dve", lambda e: e.tensor_copy(out=kf[:, :], in_=ki[:, :]), reads=[b], writes=[b])
                    kb.op("dve", lambda e: e.scalar_tensor_tensor(out=ang[:, :], in0=kf[:, :], scalar=-TWO_PI, in1=ang[:, :],
                                                                  op0=ALU.mult, op1=ALU.add), reads=[b], writes=[b])
                    kb.op("dve", lambda e: e.tensor_scalar(out=kf[:, :], in0=ang[:, :], scalar1=float(np.pi), scalar2=-TWO_PI,
                                                           op0=ALU.is_gt, op1=ALU.mult), reads=[b], writes=[b])
                    kb.op("dve", lambda e: e.tensor_tensor(out=ang[:, :], in0=ang[:, :], in1=kf[:, :], op=ALU.add), reads=[b], writes=[b])
                    kb.op("act", lambda e: e.activation(out=dst, in_=ang[:, :], func=AF.Sin), reads=[b], writes=[self.cb])

    def next_ps(self):
        i = self.ps_i
        self.ps_i = (i + 1) % 8
        return self.psum[i], self.psum_b[i]


def hT_view(big, k, t0, t1):
    return big[:, k * T + t0: k * T + t1]


def norm_phase(P, sc, big, big_b, xT, gcol0):
    with P.nc.named_scope("norm"):
        _norm_phase(P, sc, big, big_b, xT, gcol0)


def _norm_phase(P, sc, big, big_b, xT, gcol0):
    kb = P.kb
    xins = [sc.sb("xin", [128, KC, 256], F32) for _ in range(2)]; xins_b = [Buf() for _ in range(2)]
    sq = sc.sb("sq", [128, KC, 256], BF16); sq_b = Buf()
    rstd = sc.sb("rstd", [128, 256], F32); rstd_b = Buf()
    xv = xT.rearrange("(k p) t -> p k t", p=128)
    for t8 in range(8):
        t0, t1 = t8 * 256, (t8 + 1) * 256
        tt = t8 // 2
        xin, xin_b = xins[t8 % 2], xins_b[t8 % 2]
        kb.dma("sp", xin[:, :, :], xv[:, :, t0:t1], writes=[xin_b])
        kb.op("act", lambda e: e.activation(out=sq[:, :, :], in_=xin[:, :, :], func=AF.Square), reads=[xin_b], writes=[sq_b])
        ps, ps_b = P.next_ps()
        for k in range(KC):
            kb.op("pe", lambda e: e.matmul(ps[:, 0:256], lhsT=P.onesD[:, :], rhs=sq[:, k, :], start=(k == 0), stop=(k == KC - 1)),
                  reads=[P.cb, sq_b], writes=[ps_b], sig=(k == KC - 1))
        kb.op("act", lambda e: e.activation(out=rstd[:, :], in_=ps[:, 0:256], func=AF.Sqrt, bias=P.epsc[:, :], scale=1.0),
              reads=[ps_b, P.cb], writes=[rstd_b])
        kb.op("dve", lambda e: e.reciprocal(out=rstd[:, :], in_=rstd[:, :]), reads=[rstd_b], writes=[rstd_b])
        for k in range(KC):
            kb.op("dve", lambda e: e.scalar_tensor_tensor(out=hT_view(big, k, t0, t1), in0=xin[:, k, :],
                                                          scalar=P.pp[:, gcol0 + k:gcol0 + k + 1], in1=rstd[:, :],
                                                          op0=ALU.mult, op1=ALU.mult),
                  reads=[xin_b, P.pp_b, rstd_b], writes=[big_b[tt]])


def ffn_phase(P, xT_in, xT_out, gcol0, w13, w2, actT, out_b):
    with P.nc.named_scope("ffn%d" % gcol0):
        _ffn_phase(P, xT_in, xT_out, gcol0, w13, w2, actT, out_b)


def _ffn_phase(P, xT_in, xT_out, gcol0, w13, w2, actT, out_b):
    kb = P.kb
    actT_b = Buf("actT")
    with Scope(P) as sc:
        big = sc.sb("big", [128, NJ * 1024], BF16)
        big_b = [Buf() for _ in range(4)]
        bigA_b = Buf()
        with Scope(P) as sc2:
            norm_phase(P, sc2, big, big_b, xT_in, gcol0)
        wa = [sc.sb("wa", [128, 2, KC, 128], BF16) for _ in range(2)]; wa_b = [Buf() for _ in range(2)]
        w2s = [sc.sb("w2s", [128, NJ, 128], BF16) for _ in range(2)]; w2s_b = [Buf() for _ in range(2)]
        sg = [sc.sb("sg", [128, 512], BF16) for _ in range(2)]; sg_b = [Buf() for _ in range(2)]
        actt = [sc.sb("actt", [128, T], BF16) for _ in range(2)]; actt_b = [Buf() for _ in range(2)]
        xr = [sc.sb("xr", [128, 1024], F32) for _ in range(2)]; xr_b = [Buf() for _ in range(2)]
        w13v = w13.rearrange("(k p) c -> p k c", p=128)
        for j in range(NJ):
            w, w_b = wa[j % 2], wa_b[j % 2]
            kb.dma("pool", w[:, 0, :, :], w13v[:, :, j * 128:(j + 1) * 128], writes=[w_b])
            kb.dma("pool", w[:, 1, :, :], w13v[:, :, DFF + j * 128: DFF + (j + 1) * 128], writes=[w_b])
            at, at_b = actt[j % 2], actt_b[j % 2]
            for tt in range(4):
                t0, t1 = tt * 512, (tt + 1) * 512
                pg, pg_b = P.next_ps()
                pu, pu_b = P.next_ps()
                for k in range(KC):
                    kb.op("pe", lambda e: e.matmul(pg[:, :], lhsT=w[:, 0, k, :], rhs=hT_view(big, k, t0, t1),
                                                   start=(k == 0), stop=(k == KC - 1)),
                          reads=[w_b, big_b[tt]], writes=[pg_b], sig=(k == KC - 1))
                for k in range(KC):
                    kb.op("pe", lambda e: e.matmul(pu[:, :], lhsT=w[:, 1, k, :], rhs=hT_view(big, k, t0, t1),
                                                   start=(k == 0), stop=(k == KC - 1)),
                          reads=[w_b, big_b[tt]], writes=[pu_b], sig=(k == KC - 1))
                s_, s_b = sg[tt % 2], sg_b[tt % 2]
                kb.op("act", lambda e: e.activation(out=s_[:, :], in_=pg[:, :], func=AF.Silu), reads=[pg_b], writes=[s_b])
                kb.op("dve", lambda e: e.tensor_tensor(out=at[:, t0:t1], in0=pu[:, :], in1=s_[:, :], op=ALU.mult),
                      reads=[pu_b, s_b], writes=[at_b])
            kb.dma("sp", actT[j * 128:(j + 1) * 128, :], at[:, :], reads=[at_b], writes=[actT_b])
        actv = actT.rearrange("(j p) t -> p j t", p=128)
        w2v = w2.rearrange("(j p) c -> p j c", p=128)
        A = big
        for th in range(2):
            h0 = th * 1024
            for jj in range(0, NJ, 11):
                kb.dma("sp", A[:, jj * 1024:(jj + 11) * 1024].rearrange("p (j t) -> p j t", t=1024),
                       actv[:, jj:jj + 11, h0:h0 + 1024], reads=[actT_b], writes=[bigA_b] + big_b)
            for m in range(KC):
                ws, ws_b = w2s[m % 2], w2s_b[m % 2]
                for jj in range(0, NJ, 11):
                    kb.dma("pool", ws[:, jj:jj + 11, :], w2v[:, jj:jj + 11, m * 128:(m + 1) * 128], writes=[ws_b])
                x_, x_b = xr[m % 2], xr_b[m % 2]
                kb.dma("sp", x_[:, :], xT_in[m * 128:(m + 1) * 128, h0:h0 + 1024], writes=[x_b])
                p0, p0_b = P.next_ps()
                p1, p1_b = P.next_ps()
                for j in range(NJ):
                    kb.op("pe", lambda e: e.matmul(p0[:, :], lhsT=ws[:, j, :], rhs=A[:, j * 1024: j * 1024 + 512],
                                                   start=(j == 0), stop=(j == NJ - 1)),
                          reads=[ws_b, bigA_b], writes=[p0_b], sig=(j == NJ - 1))
                for j in range(NJ):
                    kb.op("pe", lambda e: e.matmul(p1[:, :], lhsT=ws[:, j, :], rhs=A[:, j * 1024 + 512: (j + 1) * 1024],
                                                   start=(j == 0), stop=(j == NJ - 1)),
                          reads=[ws_b, bigA_b], writes=[p1_b], sig=(j == NJ - 1))
                o_, o_b = x_, x_b
                kb.op("dve", lambda e: e.scalar_tensor_tensor(out=o_[:, 0:512], in0=p0[:, :], scalar=0.5, in1=x_[:, 0:512],
                                                              op0=ALU.mult, op1=ALU.add), reads=[p0_b, x_b], writes=[o_b])
                kb.op("dve", lambda e: e.scalar_tensor_tensor(out=o_[:, 512:1024], in0=p1[:, :], scalar=0.5, in1=x_[:, 512:1024],
                                                              op0=ALU.mult, op1=ALU.add), reads=[p1_b, x_b], writes=[o_b])
                kb.dma("sp", xT_out[m * 128:(m + 1) * 128, h0:h0 + 1024], o_[:, :], reads=[o_b], writes=[out_b])


def proj_fm(P, big, big_b, wv, cols, evac, wpool, wpool_b):
    kb = P.kb
    for ci, c0 in enumerate(cols):
        w, w_b = wpool[ci % len(wpool)], wpool_b[ci % len(wpool)]
        kb.dma("pool", w[:, :, :], wv[:, :, c0:c0 + 128], writes=[w_b])
        for tt in range(4):
            t0, t1 = tt * 512, (tt + 1) * 512
            ps, ps_b = P.next_ps()
            for k in range(KC):
                kb.op("pe", lambda e: e.matmul(ps[:, :], lhsT=w[:, k, :], rhs=hT_view(big, k, t0, t1),
                                               start=(k == 0), stop=(k == KC - 1)),
                      reads=[w_b, big_b[tt]], writes=[ps_b], sig=(k == KC - 1))
            evac(ci, tt, ps, ps_b)


def proj_tm_load(P, sc, wv, c0):
    kb = P.kb
    ws = sc.sb("wtm", [128, KC, 512], BF16); ws_b = Buf()
    for k4 in range(0, KC, 4):
        for cc in range(0, 512, 128):
            kb.dma("pool", ws[:, k4:k4 + 4, cc:cc + 128], wv[:, k4:k4 + 4, c0 + cc:c0 + cc + 128], writes=[ws_b])
    return ws, ws_b


def proj_tm(P, sc, big, big_b, wv, c0, evac, pre=None):
    kb = P.kb
    ws, ws_b = pre if pre is not None else proj_tm_load(P, sc, wv, c0)
    for i in range(16):
        ps, ps_b = P.next_ps()
        for k in range(KC):
            kb.op("pe", lambda e: e.matmul(ps[:, :], lhsT=hT_view(big, k, i * 128, (i + 1) * 128), rhs=ws[:, k, :],
                                           start=(k == 0), stop=(k == KC - 1)),
                  reads=[ws_b, big_b[i // 4]], writes=[ps_b], sig=(k == KC - 1))
        evac(i, ps, ps_b)


def qk_evac_factory(P, sc, ty, gcol, out_dram):
    kb = P.kb
    NB = 3
    raw = [sc.sb("qraw", [128, 512], F32) for _ in range(NB)]; raw_b = [Buf() for _ in range(NB)]
    sq = [sc.sb("qsq", [128, 512], BF16) for _ in range(NB)]; sq_b = [Buf() for _ in range(NB)]
    rs = [sc.sb("qrs", [128, 512], F32) for _ in range(NB)]; rs_b = [Buf() for _ in range(NB)]
    kn = [sc.sb("qkn", [128, 512], BF16) for _ in range(NB)]; kn_b = [Buf() for _ in range(NB)]
    t1 = [sc.sb("qt1", [128, 512], F32) for _ in range(NB)]; t1_b = [Buf() for _ in range(NB)]
    ot = [sc.sb("qot", [128, T], BF16) for _ in range(2)]; ot_b = [Buf() for _ in range(2)]
    ones_g = P.ccb[:, CC_O128:CC_O128 + 128] if ty == 0 else P.ccb[:, CC_O64:CC_O64 + 128]
    Rm = P.ccb[:, CC_RC:CC_RC + 128] if ty == 0 else P.ccb[:, CC_RD:CC_RD + 128]
    state = {"n": 0}
    ob = Buf()
    q1, q2 = [], []

    def stage1(it):
        ci, tt, ps, ps_b, i = it
        kb.op("act", lambda e: e.activation(out=raw[i][:, :], in_=ps[:, :], func=AF.Copy), reads=[ps_b], writes=[raw_b[i]])
        kb.op("act", lambda e: e.activation(out=sq[i][:, :], in_=ps[:, :], func=AF.Square), reads=[ps_b], writes=[sq_b[i]])
        p2, p2_b = P.next_ps()
        kb.op("pe", lambda e: e.matmul(p2[:, :], lhsT=ones_g, rhs=sq[i][:, :], start=True, stop=True),
              reads=[P.cb, sq_b[i]], writes=[p2_b])
        kb.op("act", lambda e: e.activation(out=rs[i][:, :], in_=p2[:, :], func=AF.Sqrt, bias=P.epsc[:, :], scale=1.0),
              reads=[p2_b, P.cb], writes=[rs_b[i]])
        kb.op("dve", lambda e: e.reciprocal(out=rs[i][:, :], in_=rs[i][:, :]), reads=[rs_b[i]], writes=[rs_b[i]])
        kb.op("dve", lambda e: e.scalar_tensor_tensor(out=kn[i][:, :], in0=raw[i][:, :], scalar=P.pp[:, gcol:gcol + 1],
                                                      in1=rs[i][:, :], op0=ALU.mult, op1=ALU.mult),
              reads=[raw_b[i], rs_b[i], P.pp_b], writes=[kn_b[i]])

    def stage2(it):
        ci, tt, ps, ps_b, i = it
        t0, t1_ = tt * 512, (tt + 1) * 512
        o, o_b = ot[ci % 2], ot_b[ci % 2]
        p3, p3_b = P.next_ps()
        kb.op("pe", lambda e: e.matmul(p3[:, :], lhsT=Rm, rhs=kn[i][:, :], start=True, stop=True),
              reads=[P.cb, kn_b[i]], writes=[p3_b])
        kb.op("dve", lambda e: e.tensor_tensor(out=t1[i][:, :], in0=kn[i][:, :], in1=P.cs[:, 2 * ty, t0:t1_], op=ALU.mult),
              reads=[kn_b[i], P.cb], writes=[t1_b[i]])
        kb.op("dve", lambda e: e.tensor_tensor(out=rs[i][:, :], in0=p3[:, :], in1=P.cs[:, 2 * ty + 1, t0:t1_], op=ALU.mult),
              reads=[p3_b, P.cb], writes=[rs_b[i]])
        kb.op("pool", lambda e: e.tensor_tensor(out=o[:, t0:t1_], in0=t1[i][:, :], in1=rs[i][:, :], op=ALU.add),
              reads=[t1_b[i], rs_b[i]], writes=[o_b])
        if tt == 3:
            kb.dma("sp", out_dram[ci * 128:(ci + 1) * 128, :], o[:, :], reads=[o_b], writes=[ob])

    def step():
        if q2:
            stage2(q2.pop(0))
        if q1:
            it = q1.pop(0)
            stage1(it)
            q2.append(it)

    def evac(ci, tt, ps, ps_b):
        n = state["n"]; state["n"] += 1
        step()
        q1.append((ci, tt, ps, ps_b, n % NB))

    def flush():
        while q1 or q2:
            step()
    return evac, flush, ob


def launchA_body(P, d):
    kb = P.kb
    out_b = Buf("x1")
    ffn_phase(P, d["xT"], d["x1T"], PP_G1, d["w13a"], d["w2a"], d["actT"], out_b)
    kb.barrier()
    exp_b = Buf("exports")
    wv = d["w_in"].rearrange("(k p) c -> p k c", p=128)
    _sA = P.nc.enter_named_scope("pA", False)[0]
    with Scope(P) as sc:
        big = sc.sb("big", [128, KC * T], BF16)
        big_b = [Buf() for _ in range(4)]
        with Scope(P) as sc2:
            norm_phase(P, sc2, big, big_b, d["x1T"], PP_GM)
        wpool = [sc.sb("wp", [128, KC, 128], BF16) for _ in range(3)]; wpool_b = [Buf() for _ in range(3)]
        sc_outer = sc
        sc = Scope(P); sc.__enter__()
        a_sb = sc.sb("a_sb", [128, T], F32); a_b = Buf()
        sgt = [sc.sb("sgt", [128, 512], F32) for _ in range(2)]; sgt_b = [Buf() for _ in range(2)]
        z_sb = [sc.sb("z_sb", [128, T], F32) for _ in range(2)]; z_b = [Buf() for _ in range(2)]

        def evac_z(ci, tt, ps, ps_b):
            t0, t1 = tt * 512, (tt + 1) * 512
            c = ci // 2
            if ci % 2 == 0:
                kb.op("act", lambda e: e.activation(out=a_sb[:, t0:t1], in_=ps[:, :], func=AF.Copy), reads=[ps_b], writes=[a_b])
            else:
                s_, s_b = sgt[tt % 2], sgt_b[tt % 2]
                kb.op("act", lambda e: e.activation(out=s_[:, :], in_=ps[:, :], func=AF.Sigmoid), reads=[ps_b], writes=[s_b])
                kb.op("dve", lambda e: e.tensor_tensor(out=z_sb[c % 2][:, t0:t1], in0=a_sb[:, t0:t1], in1=s_[:, :], op=ALU.mult),
                      reads=[a_b, s_b], writes=[z_b[c % 2]])
                if tt == 3:
                    kb.dma("sp", d["zT"][c * 128:(c + 1) * 128, :], z_sb[c % 2][:, :], reads=[z_b[c % 2]], writes=[exp_b])
                    kb.dma("sp", d["halo"][c * 128:(c + 1) * 128, :], z_sb[c % 2][:, T - 32:T], reads=[z_b[c % 2]], writes=[exp_b])
        cols = []
        for c in range(4):
            cols += [C_A + c * 128, C_AG + c * 128]
        proj_fm(P, big, big_b, wv, cols, evac_z, wpool, wpool_b)
        sc.__exit__(None, None, None)
        sc = Scope(P); sc.__enter__()
        ev, fl, ob1 = qk_evac_factory(P, sc, 0, PP_QKG + 1, d["kCT"])
        proj_fm(P, big, big_b, wv, [C_CK + h * 128 for h in range(4)], ev, wpool, wpool_b)
        fl()
        sc.__exit__(None, None, None)
        sc = Scope(P); sc.__enter__()
        ev, fl, ob2 = qk_evac_factory(P, sc, 1, PP_QKG + 3, d["kDT"])
        proj_fm(P, big, big_b, wv, [C_DK + h * 128 for h in range(4)], ev, wpool, wpool_b)
        fl()
        sc.__exit__(None, None, None)
        sc = Scope(P); sc.__enter__()
        vt = [sc.sb("vt", [128, 512], BF16) for _ in range(2)]; vt_b = [Buf() for _ in range(2)]
        pres = {c0: proj_tm_load(P, sc, wv, c0) for c0 in (C_CV, C_DV)}
        for c0, name in ((C_CV, "vC"), (C_DV, "vD")):
            def evac_v(i, ps, ps_b, name=name):
                kb.op("act", lambda e: e.activation(out=vt[i % 2][:, :], in_=ps[:, :], func=AF.Copy), reads=[ps_b], writes=[vt_b[i % 2]])
                kb.dma("sp", d[name][i * 128:(i + 1) * 128, :], vt[i % 2][:, :], reads=[vt_b[i % 2]], writes=[exp_b])
            proj_tm(P, sc, big, big_b, wv, c0, evac_v, pre=pres[c0])
        sc.__exit__(None, None, None)
        P.nc.leave_named_scope("pA", _sA, False)
        if "after_exports" in d:
            d["after_exports"]()
        if "projB" in d:
            d["projB"](big, big_b, wv, wpool, wpool_b)
    return [out_b, exp_b, ob1, ob2]


def launchB_body(P, d, after_phase1=None):
    kb = P.kb
    nc = P.nc
    wv = d["w_in"].rearrange("(k p) c -> p k c", p=128)
    qCT, qDT, gT, yT, x2T = d["qCT"], d["qDT"], d["gT"], d["yT"], d["x2T"]
    sb_ = Buf("scratch")
    ccf, ccb, pp = P.ccf, P.ccb, P.pp
    flagb = ccf[:, CC_FLAG + 1:CC_FLAG + 2]
    _cur = ['p1proj', nc.enter_named_scope('p1proj', False)[0]]
    def conv_main(sc):
        zb = [sc.sb("zb", [128, 32 + T], F32) for _ in range(2)]; zb_b = [Buf() for _ in range(2)]
        acc = sc.sb("cacc", [128, 4, T], F32); acc_b = [Buf() for _ in range(4)]
        for c in range(4):
            z, z_b = zb[c % 2], zb_b[c % 2]
            kb.dma("sp", z[:, 30:30 + T], d["zT"][c * 128:(c + 1) * 128, :], writes=[z_b])
            kb.op("dve", lambda e: e.memset(z[:, 0:30], 0.0), writes=[z_b])
            wc = PP_CW + c * 31
            kb.op("dve", lambda e: e.tensor_scalar(out=acc[:, c, :], in0=z[:, 0:T], scalar1=pp[:, wc:wc + 1], scalar2=pp[:, PP_CB + c:PP_CB + c + 1],
                                                   op0=ALU.mult, op1=ALU.add), reads=[z_b, P.pp_b], writes=[acc_b[c]])
            for k in range(1, 31):
                kb.op("dve", lambda e: e.scalar_tensor_tensor(out=acc[:, c, :], in0=z[:, k:k + T], scalar=pp[:, wc + k:wc + k + 1],
                                                              in1=acc[:, c, :], op0=ALU.mult, op1=ALU.add),
                      reads=[z_b, P.pp_b, acc_b[c]], writes=[acc_b[c]])
        return acc, acc_b

    def conv_finish(sc, cst):
        acc, acc_b = cst
        zh = sc.sb("zh", [128, 4, 64], F32); zh_b = Buf()
        cor = sc.sb("ccor", [128, 4, 32], F32); cor_b = [Buf() for _ in range(4)]
        kb.op("dve", lambda e: e.memset(zh[:, :, :], 0.0), writes=[zh_b])
        for c in range(4):
            kb.dma("sp", zh[:, c, 0:30], d["phalo"][c * 128:(c + 1) * 128, 2:32], writes=[zh_b])
        kb.op("dve", lambda e: e.tensor_scalar(out=zh[:, :, 0:30], in0=zh[:, :, 0:30], scalar1=ccf[:, CC_FLAG:CC_FLAG + 1], scalar2=None,
                                               op0=ALU.mult), reads=[zh_b, P.cb], writes=[zh_b])
        for c in range(4):
            wc = PP_CW + c * 31
            kb.op("dve", lambda e: e.tensor_scalar(out=cor[:, c, 0:30], in0=zh[:, c, 0:30], scalar1=pp[:, wc:wc + 1], scalar2=None,
                                                   op0=ALU.mult), reads=[zh_b, P.pp_b], writes=[cor_b[c]])
            for k in range(1, 30):
                kb.op("dve", lambda e: e.scalar_tensor_tensor(out=cor[:, c, 0:30], in0=zh[:, c, k:k + 30], scalar=pp[:, wc + k:wc + k + 1],
                                                              in1=cor[:, c, 0:30], op0=ALU.mult, op1=ALU.add),
                      reads=[zh_b, P.pp_b, cor_b[c]], writes=[cor_b[c]])
            kb.op("dve", lambda e: e.tensor_tensor(out=acc[:, c, 0:30], in0=acc[:, c, 0:30], in1=cor[:, c, 0:30], op=ALU.add),
                  reads=[cor_b[c], acc_b[c]], writes=[acc_b[c]])
        sqc = [sc.sb("sqc", [128, 512], F32) for _ in range(2)]; sqc_b = [Buf() for _ in range(2)]
        mean = sc.sb("cmean", [128, 512], F32); mean_b = Buf()
        var = sc.sb("cvar", [128, 512], F32); var_b = Buf()
        tq = [sc.sb("ctq", [128, 512], F32) for _ in range(2)]; tq_b = [Buf() for _ in range(2)]
        yat = [sc.sb("yat", [128, 512], BF16) for _ in range(2)]; yat_b = [Buf() for _ in range(2)]
        for tt in range(4):
            t0, t1 = tt * 512, (tt + 1) * 512
            pmn, pmn_b = P.next_ps()
            for c in range(4):
                kb.op("pe", lambda e: e.matmul(pmn[:, :], lhsT=P.onesLN[:, :], rhs=acc[:, c, t0:t1], start=(c == 0), stop=(c == 3)),
                      reads=[P.cb, acc_b[c]], writes=[pmn_b], sig=(c == 3))
            pex, pex_b = P.next_ps()
            for c in range(4):
                kb.op("act", lambda e: e.activation(out=sqc[c % 2][:, :], in_=acc[:, c, t0:t1], func=AF.Square),
                      reads=[acc_b[c]], writes=[sqc_b[c % 2]])
                kb.op("pe", lambda e: e.matmul(pex[:, :], lhsT=P.onesLN[:, :], rhs=sqc[c % 2][:, :], start=(c == 0), stop=(c == 3)),
                      reads=[P.cb, sqc_b[c % 2]], writes=[pex_b], sig=True)
            kb.op("act", lambda e: e.activation(out=mean[:, :], in_=pmn[:, :], func=AF.Copy), reads=[pmn_b], writes=[mean_b])
            kb.op("dve", lambda e: e.tensor_tensor(out=var[:, :], in0=mean[:, :], in1=mean[:, :], op=ALU.mult), reads=[mean_b], writes=[var_b])
            kb.op("dve", lambda e: e.tensor_tensor(out=var[:, :], in0=pex[:, :], in1=var[:, :], op=ALU.subtract), reads=[pex_b, var_b], writes=[var_b])
            kb.op("act", lambda e: e.activation(out=var[:, :], in_=var[:, :], func=AF.Sqrt, bias=P.epsc[:, :], scale=1.0),
                  reads=[var_b, P.cb], writes=[var_b])
            kb.op("dve", lambda e: e.reciprocal(out=var[:, :], in_=var[:, :]), reads=[var_b], writes=[var_b])
            for c in range(4):
                q_, q_b = tq[c % 2], tq_b[c % 2]
                kb.op("pool", lambda e: e.tensor_tensor(out=q_[:, :], in0=acc[:, c, t0:t1], in1=mean[:, :], op=ALU.subtract),
                      reads=[acc_b[c], mean_b], writes=[q_b])
                kb.op("dve", lambda e: e.tensor_tensor(out=q_[:, :], in0=q_[:, :], in1=var[:, :], op=ALU.mult), reads=[q_b, var_b], writes=[q_b])
                y_, y_b = yat[c % 2], yat_b[c % 2]
                kb.op("act", lambda e: e.activation(out=y_[:, :], in_=q_[:, :], func=AF.Silu, scale=pp[:, PP_CG + c:PP_CG + c + 1],
                                                    bias=pp[:, PP_CBE + c:PP_CBE + c + 1]), reads=[q_b, P.pp_b], writes=[y_b])
                kb.dma("sp", yT[c * 128:(c + 1) * 128, t0:t1], y_[:, :], reads=[y_b], writes=[sb_])

    def projB(big, big_b, wv, wpool, wpool_b):
        with P.nc.named_scope("pB"):
            _projB(big, big_b, wv, wpool, wpool_b)

    def _projB(big, big_b, wv, wpool, wpool_b):
        with Scope(P) as sc:
            ev, fl, ob = qk_evac_factory(P, sc, 0, PP_QKG + 0, qCT)
            proj_fm(P, big, big_b, wv, [C_CQ + h * 128 for h in range(4)], ev, wpool, wpool_b)
            fl()
        with Scope(P) as sc:
            ev, fl, ob = qk_evac_factory(P, sc, 1, PP_QKG + 2, qDT)
            proj_fm(P, big, big_b, wv, [C_DQ + h * 128 for h in range(4)], ev, wpool, wpool_b)
            fl()
        _sg = P.nc.enter_named_scope("pBgate", False)[0]
        with Scope(P) as sc:
            cst = conv_main(sc) if d.get("conv_in_proj") else None
            gt = [sc.sb("gt", [128, T], BF16) for _ in range(2)]; gt_b = [Buf() for _ in range(2)]

            def evac_g(ci, tt, ps, ps_b):
                t0, t1 = tt * 512, (tt + 1) * 512
                kb.op("act", lambda e: e.activation(out=gt[ci % 2][:, t0:t1], in_=ps[:, :], func=AF.Sigmoid),
                      reads=[ps_b], writes=[gt_b[ci % 2]])
                if tt == 3:
                    kb.dma("sp", gT[ci * 128:(ci + 1) * 128, :], gt[ci % 2][:, :], reads=[gt_b[ci % 2]], writes=[sb_])
            proj_fm(P, big, big_b, wv, [C_G + i * 128 for i in range(64)], evac_g, wpool, wpool_b)
            if cst is not None:
                after_phase1()
                conv_finish(sc, cst)
        P.nc.leave_named_scope("pBgate", _sg, False)
        with Scope(P) as sc:
            uT = sc.sb("uT", [128, 4, T], BF16); uT_b = Buf()
            ybT = sc.sb("ybT", [128, 4, T], BF16); ybT_b = Buf()

            def evac_u(ci, tt, ps, ps_b):
                t0, t1 = tt * 512, (tt + 1) * 512
                kb.op("act", lambda e: e.activation(out=uT[:, ci, t0:t1], in_=ps[:, :], func=AF.Gelu), reads=[ps_b], writes=[uT_b])
            pre_v = proj_tm_load(P, sc, wv, C_V)
            proj_fm(P, big, big_b, wv, [C_U + c * 128 for c in range(4)], evac_u, wpool, wpool_b)
            wtt = sc.sb("wtt", [128, 4, 128], BF16); wtt_b = Buf()
            for g in range(4):
                kb.op("dve", lambda e: e.tensor_tensor(out=wtt[:, g, :], in0=pp[:, PP_SW + g * 128:PP_SW + (g + 1) * 128],
                                                       in1=ccf[:, CC_M0:CC_M0 + 128], op=ALU.mult),
                      reads=[P.pp_b, P.cb], writes=[wtt_b])
            bsr = sc.sb("bsr", [128, 512], BF16); bsr_b = Buf()
            kb.op("dve", lambda e: e.tensor_copy(out=bsr[:, :], in_=pp[:, PP_SBS:PP_SBS + 512]), reads=[P.pp_b], writes=[bsr_b])
            vg = [sc.sb("vg", [128, 512], F32) for _ in range(2)]; vg_b = [Buf() for _ in range(2)]
            junk = sc.sb("junk", [128, 512], F32); junk_b = Buf()
            st = [sc.sb("st", [128, 8], F32) for _ in range(2)]; st_b = [Buf() for _ in range(2)]
            vln = [sc.sb("vln", [128, 512], BF16) for _ in range(2)]; vln_b = [Buf() for _ in range(2)]

            svq = []

            def evac_sv(i, ps, ps_b):
                v, v_b = vg[i % 2], vg_b[i % 2]
                s, s_b = st[i % 2], st_b[i % 2]
                kb.op("act", lambda e: e.activation(out=v[:, :], in_=ps[:, :], func=AF.Gelu, accum_out=s[:, 0:1]),
                      reads=[ps_b], writes=[v_b, s_b])
                kb.op("act", lambda e: e.activation(out=junk[:, :], in_=v[:, :], func=AF.Square, accum_out=s[:, 1:2]),
                      reads=[v_b], writes=[junk_b, s_b])
                kb.op("dve", lambda e: e.tensor_scalar(out=s[:, 2:3], in0=s[:, 0:1], scalar1=1.0 / 512, scalar2=None, op0=ALU.mult),
                      reads=[s_b], writes=[s_b])
                kb.op("dve", lambda e: e.tensor_tensor(out=s[:, 3:4], in0=s[:, 2:3], in1=s[:, 2:3], op=ALU.mult), reads=[s_b], writes=[s_b])
                kb.op("dve", lambda e: e.scalar_tensor_tensor(out=s[:, 4:5], in0=s[:, 1:2], scalar=1.0 / 512, in1=s[:, 3:4],
                                                              op0=ALU.mult, op1=ALU.subtract), reads=[s_b], writes=[s_b])
                kb.op("act", lambda e: e.activation(out=s[:, 5:6], in_=s[:, 4:5], func=AF.Sqrt, bias=P.epsc[:, :], scale=1.0),
                      reads=[s_b, P.cb], writes=[s_b])
                kb.op("dve", lambda e: e.reciprocal(out=s[:, 6:7], in_=s[:, 5:6]), reads=[s_b], writes=[s_b])
                kb.op("dve", lambda e: e.tensor_scalar(out=v[:, :], in0=v[:, :], scalar1=s[:, 2:3], scalar2=s[:, 6:7],
                                                       op0=ALU.subtract, op1=ALU.mult), reads=[v_b, s_b], writes=[v_b])
                kb.op("pool", lambda e: e.tensor_tensor(out=v[:, :], in0=v[:, :], in1=pp[:, PP_SG:PP_SG + 512], op=ALU.mult),
                      reads=[v_b, P.pp_b], writes=[v_b])
                vl, vl_b = vln[i % 2], vln_b[i % 2]
                kb.op("pool", lambda e: e.tensor_tensor(out=vl[:, :], in0=v[:, :], in1=pp[:, PP_SB:PP_SB + 512], op=ALU.add),
                      reads=[v_b, P.pp_b], writes=[vl_b])
                if svq:
                    sv_stage2(*svq.pop(0))
                svq.append((i, vl, vl_b))

            def sv_stage2(i, vl, vl_b):
                pm, pm_b = P.next_ps()
                for g in range(4):
                    kb.op("pe", lambda e: e.matmul(pm[:, g * 128:(g + 1) * 128], lhsT=vl[:, g * 128:(g + 1) * 128], rhs=wtt[:, g, :],
                                                   start=True, stop=False), reads=[vl_b, wtt_b], writes=[pm_b], sig=False)
                    kb.op("pe", lambda e: e.matmul(pm[:, g * 128:(g + 1) * 128], lhsT=ccb[:, CC_O128:CC_O128 + 128], rhs=bsr[:, g * 128:(g + 1) * 128],
                                                   start=False, stop=True), reads=[P.cb, bsr_b], writes=[pm_b], sig=(g == 3))
                kb.op("dve", lambda e: e.tensor_tensor(out=ybT[:, :, i * 128:(i + 1) * 128], in0=uT[:, :, i * 128:(i + 1) * 128],
                                                       in1=pm[:, :].rearrange("p (g t) -> p g t", g=4), op=ALU.mult),
                      reads=[uT_b, pm_b], writes=[ybT_b])
            proj_tm(P, sc, big, big_b, wv, C_V, evac_sv, pre=pre_v)
            while svq:
                sv_stage2(*svq.pop(0))
            for g in range(4):
                kb.dma("sp", yT[512 + g * 128: 512 + (g + 1) * 128, :], ybT[:, g, :], reads=[ybT_b], writes=[sb_])

    if d.get("only_projB"):
        return projB
    if not d.get("p1_done"):
        with Scope(P) as sc0:
            big = sc0.sb("big", [128, KC * T], BF16)
            big_b = [Buf() for _ in range(4)]
            with Scope(P) as sc2:
                norm_phase(P, sc2, big, big_b, d["x1T"], PP_GM)
            wpool = [sc0.sb("wp", [128, KC, 128], BF16) for _ in range(3)]; wpool_b = [Buf() for _ in range(3)]
            projB(big, big_b, wv, wpool, wpool_b)
    if after_phase1 is not None:
        after_phase1()
    nc.leave_named_scope(_cur[0], _cur[1], False)
    _cur = ['p2conv', nc.enter_named_scope('p2conv', False)[0]]
    if not d.get("conv_done"):
        with Scope(P) as sc:
            cst = conv_main(sc)
            conv_finish(sc, cst)
    nc.leave_named_scope(_cur[0], _cur[1], False)
    _cur = ['p3dil', nc.enter_named_scope('p3dil', False)[0]]
    with Scope(P) as sc:
        kall = sc.sb("kall", [128, 4, 2 * T], BF16); kall_b = Buf()
        qc = sc.sb("qc", [128, 4, T], BF16); qc_b = Buf()
        for h in range(4):
            kb.dma("sp", kall[:, h, 0:T], d["pkCT"][h * 128:(h + 1) * 128, :], writes=[kall_b])
            kb.dma("sp", kall[:, h, T:2 * T], d["kCT"][h * 128:(h + 1) * 128, :], writes=[kall_b])
            kb.dma("sp", qc[:, h, :], qCT[h * 128:(h + 1) * 128, :], reads=[sb_], writes=[qc_b])
        num = sc.sb("num", [128, 4, T], F32); num_b = Buf()
        den = sc.sb("den", [128, 4, T], F32); den_b = Buf()
        m0r = sc.sb("m0r", [128, 4, 128], BF16); m1r = sc.sb("m1r", [128, 4, 128], BF16); mr_b = Buf()
        for g in range(4):
            kb.op("dve", lambda e: e.tensor_copy(out=m0r[:, g, :], in_=ccb[:, CC_M0:CC_M0 + 128]), reads=[P.cb], writes=[mr_b])
            kb.op("dve", lambda e: e.tensor_copy(out=m1r[:, g, :], in_=ccb[:, CC_M1:CC_M1 + 128]), reads=[P.cb], writes=[mr_b])
        vcu = [sc.sb("vcu", [128, 512], BF16) for _ in range(2)]; vcu_b = [Buf() for _ in range(2)]
        vpr = [sc.sb("vpr", [128, 512], BF16) for _ in range(2)]; vpr_b = [Buf() for _ in range(2)]
        ptc = [sc.sb("ptc", [128, 4, 128], BF16) for _ in range(2)]; ptc_b = [Buf() for _ in range(2)]
        ptp = [sc.sb("ptp", [128, 4, 128], BF16) for _ in range(2)]; ptp_b = [Buf() for _ in range(2)]
        SC_C = 128 ** -0.5
        n = 0
        dq = []

        def dil_pv(i, pi, sl_q, c_, c_b, p_, p_b):
            pso, pso_b = P.next_ps()
            for h in range(4):
                kb.op("pe", lambda e: e.matmul(pso[:, h * 128:(h + 1) * 128], lhsT=vcu[i][:, h * 128:(h + 1) * 128], rhs=c_[:, h, :],
                                               start=True, stop=False), reads=[vcu_b[i], c_b], writes=[pso_b], sig=False)
                kb.op("pe", lambda e: e.matmul(pso[:, h * 128:(h + 1) * 128], lhsT=vpr[i][:, h * 128:(h + 1) * 128], rhs=p_[:, h, :],
                                               start=False, stop=True), reads=[vpr_b[i], p_b], writes=[pso_b], sig=(h == 3))
            psd, psd_b = P.next_ps()
            kb.op("pe", lambda e: e.matmul(psd[:, :], lhsT=P.ones1[:, :], rhs=c_[:, :, :], start=True, stop=False),
                  reads=[P.cb, c_b], writes=[psd_b], sig=False)
            kb.op("pe", lambda e: e.matmul(psd[:, :], lhsT=P.ones1[:, :], rhs=p_[:, :, :], start=False, stop=True),
                  reads=[P.cb, p_b], writes=[psd_b], sig=True)
            dn = num[:, :, sl_q]
            dd = den[:, :, sl_q]
            po = pso[:, :].rearrange("p (g t) -> p g t", g=4)
            pd = psd[:, :].rearrange("p (g t) -> p g t", g=4)
            if pi == 0:
                kb.op("act", lambda e: e.activation(out=dn, in_=po, func=AF.Copy), reads=[pso_b], writes=[num_b])
                kb.op("act", lambda e: e.activation(out=dd, in_=pd, func=AF.Copy), reads=[psd_b], writes=[den_b])
            else:
                kb.op("dve", lambda e: e.tensor_tensor(out=dn, in0=dn, in1=po, op=ALU.add), reads=[pso_b, num_b], writes=[num_b])
                kb.op("dve", lambda e: e.tensor_tensor(out=dd, in0=dd, in1=pd, op=ALU.add), reads=[psd_b, den_b], writes=[den_b])

        for pi, dl in enumerate((1, 4, 16)):
            nb = T // (128 * dl)
            vown = d["vC"].rearrange("(l s) c -> l s c", s=dl)
            vpar = d["pvC"].rearrange("(l s) c -> l s c", s=dl)
            for r in range(dl):
                for blk in range(nb):
                    i = n % 2; n += 1
                    s0 = r + dl * 128 * blk
                    l0 = 128 * blk
                    sl_q = slice(s0, s0 + 127 * dl + 1, dl)
                    sl_kc = slice(T + s0, T + s0 + 127 * dl + 1, dl)
                    sl_kp = slice(T + s0 - 128 * dl, T + s0 - dl + 1, dl)
                    kb.dma("sp", vcu[i][:, :], vown[l0:l0 + 128, r, :], writes=[vcu_b[i]])
                    if blk == 0:
                        kb.dma("sp", vpr[i][:, :], vpar[T // dl - 128:T // dl, r, :], writes=[vpr_b[i]])
                    else:
                        kb.dma("sp", vpr[i][:, :], vown[l0 - 128:l0, r, :], writes=[vpr_b[i]])
                    psc, psc_b = P.next_ps()
                    psp, psp_b = P.next_ps()
                    for h in range(4):
                        kb.op("pe", lambda e: e.matmul(psc[:, h * 128:(h + 1) * 128], lhsT=kall[:, h, sl_kc], rhs=qc[:, h, sl_q],
                                                       start=True, stop=True), reads=[kall_b, qc_b], writes=[psc_b], sig=(h == 3))
                    for h in range(4):
                        kb.op("pe", lambda e: e.matmul(psp[:, h * 128:(h + 1) * 128], lhsT=kall[:, h, sl_kp], rhs=qc[:, h, sl_q],
                                                       start=True, stop=True), reads=[kall_b, qc_b], writes=[psp_b], sig=(h == 3))
                    c_, c_b = ptc[i], ptc_b[i]
                    p_, p_b = ptp[i], ptp_b[i]
                    kb.op("act", lambda e: e.activation(out=c_[:, :, :], in_=psc[:, :].rearrange("p (g t) -> p g t", g=4), func=AF.Exp, scale=SC_C),
                          reads=[psc_b], writes=[c_b])
                    if blk == 0:
                        kb.op("act", lambda e: e.activation(out=p_[:, :, :], in_=psp[:, :].rearrange("p (g t) -> p g t", g=4), func=AF.Exp,
                                                            scale=SC_C, bias=flagb), reads=[psp_b, P.cb], writes=[p_b])
                    else:
                        kb.op("act", lambda e: e.activation(out=p_[:, :, :], in_=psp[:, :].rearrange("p (g t) -> p g t", g=4), func=AF.Exp,
                                                            scale=SC_C), reads=[psp_b], writes=[p_b])
                    kb.op("dve", lambda e: e.tensor_tensor(out=c_[:, :, :], in0=c_[:, :, :], in1=m0r[:, :, :], op=ALU.mult),
                          reads=[c_b, mr_b], writes=[c_b])
                    kb.op("pool", lambda e: e.tensor_tensor(out=p_[:, :, :], in0=p_[:, :, :], in1=m1r[:, :, :], op=ALU.mult),
                          reads=[p_b, mr_b], writes=[p_b])
                    dq.append((i, pi, sl_q, c_, c_b, p_, p_b))
                    if len(dq) > 1:
                        dil_pv(*dq.pop(0))
        while dq:
            dil_pv(*dq.pop(0))
        yct = sc.sb("yct", [128, 4, T], BF16); yct_b = Buf()
        for h in range(4):
            kb.op("dve", lambda e: e.reciprocal(out=den[:, h, :], in_=den[:, h, :]), reads=[den_b], writes=[den_b])
            kb.op("dve", lambda e: e.tensor_tensor(out=yct[:, h, :], in0=num[:, h, :], in1=den[:, h, :], op=ALU.mult),
                  reads=[num_b, den_b], writes=[yct_b])
            kb.dma("sp", yT[1024 + h * 128:1024 + (h + 1) * 128, :], yct[:, h, :], reads=[yct_b], writes=[sb_])
    nc.leave_named_scope(_cur[0], _cur[1], False)
    _cur = ['p4diff', nc.enter_named_scope('p4diff', False)[0]]
    with Scope(P) as sc:
        kall = sc.sb("kalld", [128, 4, 2 * T], BF16); kall_b = Buf()
        qdp = [sc.sb("qdp", [128, 4, T], BF16) for _ in range(2)]; qd_b = Buf()
        vall = sc.sb("vall", [128, 32, 512], BF16); vall_b = Buf()
        kb.op("pool", lambda e: e.memset(qdp[0][64:128, :, :], 0.0), writes=[qd_b])
        kb.op("pool", lambda e: e.memset(qdp[1][0:64, :, :], 0.0), writes=[qd_b])
        for h in range(4):
            kb.dma("sp", kall[:, h, 0:T], d["pkDT"][h * 128:(h + 1) * 128, :], writes=[kall_b])
            kb.dma("sp", kall[:, h, T:2 * T], d["kDT"][h * 128:(h + 1) * 128, :], writes=[kall_b])
            kb.dma("sp", qdp[0][0:64, h, :], qDT[h * 128:h * 128 + 64, :], reads=[sb_], writes=[qd_b])
            kb.dma("sp", qdp[1][64:128, h, :], qDT[h * 128 + 64:(h + 1) * 128, :], reads=[sb_], writes=[qd_b])
        for b4 in range(0, 16, 4):
            kb.dma("sp", vall[:, b4:b4 + 4, :], d["pvD"].rearrange("(b p) c -> p b c", p=128)[:, b4:b4 + 4, :], writes=[vall_b])
            kb.dma("sp", vall[:, 16 + b4:16 + b4 + 4, :], d["vD"].rearrange("(b p) c -> p b c", p=128)[:, b4:b4 + 4, :], writes=[vall_b])
        lt = sc.sb("lt", [128, 64], F32); ls = sc.sb("ls", [128, 8], F32); ls_b = Buf()
        for j in range(2):
            kb.op("dve", lambda e: e.tensor_tensor(out=lt[:, :], in0=pp[:, PP_LAM + j * 128:PP_LAM + j * 128 + 64],
                                                   in1=pp[:, PP_LAM + j * 128 + 64:PP_LAM + j * 128 + 128], op=ALU.mult),
                  reads=[P.pp_b, ls_b], writes=[ls_b])
            kb.op("dve", lambda e: e.reduce_sum(out=ls[:, j:j + 1], in_=lt[:, :], axis=AX.X), reads=[ls_b], writes=[ls_b])
            kb.op("act", lambda e: e.activation(out=ls[:, 2 + j:3 + j], in_=ls[:, j:j + 1], func=AF.Exp), reads=[ls_b], writes=[ls_b])
        kb.op("dve", lambda e: e.tensor_tensor(out=ls[:, 4:5], in0=ls[:, 3:4], in1=ls[:, 2:3], op=ALU.subtract), reads=[ls_b], writes=[ls_b])
        kb.op("dve", lambda e: e.tensor_tensor(out=ls[:, 5:6], in0=ls[:, 4:5], in1=pp[:, PP_LI:PP_LI + 1], op=ALU.subtract),
              reads=[ls_b, P.pp_b], writes=[ls_b])
        kb.op("dve", lambda e: e.tensor_tensor(out=ls[:, 6:7], in0=pp[:, PP_SUB:PP_SUB + 1], in1=pp[:, PP_OML:PP_OML + 1], op=ALU.mult),
              reads=[ls_b, P.pp_b], writes=[ls_b])
        neglam = ls[:, 5:6]
        subs = ls[:, 6:7]
        pt = [sc.sb("ptd", [128, 512], BF16) for _ in range(6)]; pt_b = [Buf() for _ in range(6)]
        o0 = sc.sb("o0", [128, 512], F32); o1 = sc.sb("o1", [128, 512], F32); rr = sc.sb("rr", [128, 512], F32); o_b = Buf()
        osq = sc.sb("osq", [128, 512], BF16); osq_b = Buf()
        ydt = [sc.sb("ydt", [128, 512], BF16) for _ in range(2)]; ydt_b = [Buf() for _ in range(2)]
        n = 0
        si = 0
        LA = 3
        for h in range(4):
            for qt in range(4):
                q0, q1 = qt * 512, (qt + 1) * 512
                nkb = 16 + 4 * qt + 4
                accs = [(P.psum[4 + a], P.psum_b[4 + a]) for a in range(4)]
                steps = [(kbi, c) for kbi in range(nkb) for c in range(2)]
                pend = []

                def emit_pv(item):
                    kbi, c, p_, p_b = item
                    last = (kbi == nkb - 1)
                    kb.op("pe", lambda e: e.matmul(accs[c][0][:, :], lhsT=vall[:, kbi, h * 128:(h + 1) * 128], rhs=p_[:, :],
                                                   start=(kbi == 0), stop=last), reads=[vall_b, p_b], writes=[accs[c][1]], sig=last)
                    kb.op("pe", lambda e: e.matmul(accs[2 + c][0][:, :], lhsT=P.ones1[:, :], rhs=p_[:, :],
                                                   start=(kbi == 0), stop=last), reads=[P.cb, p_b], writes=[accs[2 + c][1]], sig=last)
                for (kbi, c) in steps:
                    ps, ps_b = P.psum[si % 4], P.psum_b[si % 4]; si += 1
                    kb.op("pe", lambda e: e.matmul(ps[:, :], lhsT=kall[:, h, kbi * 128:(kbi + 1) * 128],
                                                   rhs=qdp[c][:, h, q0:q1], start=True, stop=True),
                          reads=[kall_b, qd_b], writes=[ps_b])
                    p_, p_b = pt[n % 6], pt_b[n % 6]; n += 1
                    if kbi < 16:
                        kb.op("act", lambda e: e.activation(out=p_[:, :], in_=ps[:, :], func=AF.Exp, scale=0.125, bias=flagb),
                              reads=[ps_b, P.cb], writes=[p_b])
                    else:
                        kb.op("act", lambda e: e.activation(out=p_[:, :], in_=ps[:, :], func=AF.Exp, scale=0.125), reads=[ps_b], writes=[p_b])
                    jr = kbi - 16 - 4 * qt
                    if jr >= 0:
                        eng = "dve" if c == 0 else "pool"
                        kb.op(eng, lambda e: e.tensor_tensor(out=p_[:, :], in0=p_[:, :], in1=ccb[:, CC_DM + jr * 512:CC_DM + (jr + 1) * 512],
                                                             op=ALU.mult), reads=[p_b, P.cb], writes=[p_b])
                    pend.append((kbi, c, p_, p_b))
                    if len(pend) > LA:
                        emit_pv(pend.pop(0))
                while pend:
                    emit_pv(pend.pop(0))
                kb.op("dve", lambda e: e.reciprocal(out=rr[:, :], in_=accs[2][0][:, :]), reads=[accs[2][1], o_b], writes=[o_b])
                kb.op("dve", lambda e: e.tensor_tensor(out=o0[:, :], in0=accs[0][0][:, :], in1=rr[:, :], op=ALU.mult), reads=[accs[0][1], o_b], writes=[o_b])
                kb.op("dve", lambda e: e.reciprocal(out=rr[:, :], in_=accs[3][0][:, :]), reads=[accs[3][1], o_b], writes=[o_b])
                kb.op("dve", lambda e: e.tensor_tensor(out=o1[:, :], in0=accs[1][0][:, :], in1=rr[:, :], op=ALU.mult), reads=[accs[1][1], o_b], writes=[o_b])
                kb.op("dve", lambda e: e.scalar_tensor_tensor(out=o0[:, :], in0=o1[:, :], scalar=neglam, in1=o0[:, :], op0=ALU.mult, op1=ALU.add),
                      reads=[o_b, ls_b], writes=[o_b])
                kb.op("act", lambda e: e.activation(out=osq[:, :], in_=o0[:, :], func=AF.Square), reads=[o_b], writes=[osq_b])
                ps, ps_b = P.psum[si % 4], P.psum_b[si % 4]; si += 1
                kb.op("pe", lambda e: e.matmul(ps[:, :], lhsT=ccb[:, CC_O128:CC_O128 + 128], rhs=osq[:, :], start=True, stop=True),
                      reads=[P.cb, osq_b], writes=[ps_b])
                kb.op("act", lambda e: e.activation(out=rr[:, :], in_=ps[:, :], func=AF.Sqrt, bias=P.epsc[:, :], scale=1.0),
                      reads=[ps_b, P.cb, o_b], writes=[o_b])
                kb.op("dve", lambda e: e.reciprocal(out=rr[:, :], in_=rr[:, :]), reads=[o_b], writes=[o_b])
                y_, y_b = ydt[qt % 2], ydt_b[qt % 2]
                kb.op("dve", lambda e: e.scalar_tensor_tensor(out=y_[:, :], in0=o0[:, :], scalar=subs, in1=rr[:, :], op0=ALU.mult, op1=ALU.mult),
                      reads=[o_b, ls_b], writes=[y_b])
                kb.dma("sp", yT[1536 + h * 128:1536 + (h + 1) * 128, q0:q1], y_[:, :], reads=[y_b], writes=[sb_])
    nc.leave_named_scope(_cur[0], _cur[1], False)
    _cur = ['p5merge', nc.enter_named_scope('p5merge', False)[0]]
    out2_b = Buf("x2")
    with Scope(P) as sc:
        mT = sc.sb("mT", [128, KC * T], BF16); mT_b = [Buf() for _ in range(4)]
        with Scope(P) as sc1:
            yall = sc1.sb("yall", [128, KC, 1024], BF16); yall_b = Buf()
            yv = yT.rearrange("(k p) t -> p k t", p=128)
            wbv = d["w_br"].rearrange("(k p) c -> p k c", p=128)
            wp = [sc1.sb("wpb", [128, KC, 128], BF16) for _ in range(2)]; wp_b = [Buf() for _ in range(2)]
            gtl = [sc1.sb("gtl", [128, 4, 1024], BF16) for _ in range(2)]; gtl_b = [Buf() for _ in range(2)]
            ac = [sc1.sb("mac", [128, 512], F32) for _ in range(2)]; ac_b = [Buf() for _ in range(2)]
            tm = [sc1.sb("mtm", [128, 512], F32) for _ in range(2)]; tm_b = [Buf() for _ in range(2)]
            gv = gT.rearrange("(b r) t -> r b t", b=4)
            n = 0
            for th in range(2):
                for k4 in range(0, KC, 4):
                    kb.dma("sp", yall[:, k4:k4 + 4, :], yv[:, k4:k4 + 4, th * 1024:(th + 1) * 1024], reads=[sb_], writes=[yall_b])
                for m in range(KC):
                    w, w_b = wp[m % 2], wp_b[m % 2]
                    kb.dma("pool", w[:, :, :], wbv[:, :, m * 128:(m + 1) * 128], writes=[w_b])
                    gl, gl_b = gtl[m % 2], gtl_b[m % 2]
                    kb.dma("sp", gl[:, :, :], gv[m * 128:(m + 1) * 128, :, th * 1024:(th + 1) * 1024], reads=[sb_], writes=[gl_b])
                    for t2 in range(2):
                        tt = th * 2 + t2
                        t0, t1 = tt * 512, (tt + 1) * 512
                        l0, l1 = t2 * 512, (t2 + 1) * 512
                        g_, g_b = gl[:, :, l0:l1], gl_b
                        a_, a_b = ac[n % 2], ac_b[n % 2]
                        n += 1
                        for br in range(4):
                            ps, ps_b = P.next_ps()
                            for kc in range(4):
                                kb.op("pe", lambda e: e.matmul(ps[:, :], lhsT=w[:, br * 4 + kc, :], rhs=yall[:, br * 4 + kc, l0:l1],
                                                               start=(kc == 0), stop=(kc == 3)), reads=[w_b, yall_b], writes=[ps_b], sig=(kc == 3))
                            if br == 0:
                                kb.op("dve", lambda e: e.tensor_tensor(out=a_[:, :], in0=ps[:, :], in1=g_[:, 0, :], op=ALU.mult),
                                      reads=[ps_b, g_b], writes=[a_b])
                            else:
                                t_, t_b = tm[br % 2], tm_b[br % 2]
                                kb.op("dve", lambda e: e.tensor_tensor(out=t_[:, :], in0=ps[:, :], in1=g_[:, br, :], op=ALU.mult),
                                      reads=[ps_b, g_b], writes=[t_b])
                                if br < 3:
                                    kb.op("dve", lambda e: e.tensor_tensor(out=a_[:, :], in0=a_[:, :], in1=t_[:, :], op=ALU.add),
                                          reads=[a_b, t_b], writes=[a_b])
                                else:
                                    kb.op("dve", lambda e: e.tensor_tensor(out=hT_view(mT, m, t0, t1), in0=a_[:, :], in1=t_[:, :], op=ALU.add),
                                          reads=[a_b, t_b], writes=[mT_b[tt]])
        wov = d["w_out"].rearrange("(k p) c -> p k c", p=128)
        wp = [sc.sb("wpo", [128, KC, 128], BF16) for _ in range(2)]; wp_b = [Buf() for _ in range(2)]
        xr = [sc.sb("xro", [128, 1024], F32) for _ in range(2)]; xr_b = [Buf() for _ in range(2)]
        n = 0
        for m in range(KC):
            w, w_b = wp[m % 2], wp_b[m % 2]
            kb.dma("pool", w[:, :, :], wov[:, :, m * 128:(m + 1) * 128], writes=[w_b])
            for th in range(2):
                h0 = th * 1024
                x_, x_b = xr[n % 2], xr_b[n % 2]; n += 1
                kb.dma("sp", x_[:, :], d["x1T"][m * 128:(m + 1) * 128, h0:h0 + 1024], writes=[x_b])
                for hh in range(2):
                    t0 = h0 + hh * 512
                    ps, ps_b = P.next_ps()
                    for k in range(KC):
                        kb.op("pe", lambda e: e.matmul(ps[:, :], lhsT=w[:, k, :], rhs=hT_view(mT, k, t0, t0 + 512),
                                                       start=(k == 0), stop=(k == KC - 1)), reads=[w_b, mT_b[t0 // 512]], writes=[ps_b], sig=(k == KC - 1))
                    kb.op("dve", lambda e: e.tensor_tensor(out=x_[:, hh * 512:(hh + 1) * 512], in0=ps[:, :], in1=x_[:, hh * 512:(hh + 1) * 512], op=ALU.add),
                          reads=[ps_b, x_b], writes=[x_b])
                kb.dma("sp", x2T[m * 128:(m + 1) * 128, h0:h0 + 1024], x_[:, :], reads=[x_b], writes=[out2_b])
    nc.leave_named_scope(_cur[0], _cur[1], False)
    kb.barrier()
    out_b = Buf("x3")
    ffn_phase(P, x2T, d["x3T"], PP_G2, d["w13b"], d["w2b"], d["actT"], out_b)
    return [out_b]


import math
import ml_dtypes
THETA = 500000.0
_BF = ml_dtypes.bfloat16
NLAYER = 4
WNAMES = ("w13a", "w2a", "w_in", "w_br", "w_out", "w13b", "w2b")
WSHAPES = {"w13a": [D, 2 * DFF], "w2a": [DFF, D], "w_in": [D, INC], "w_br": [D, D], "w_out": [D, D],
           "w13b": [D, 2 * DFF], "w2b": [DFF, D]}


def build_fused(nlayer=NLAYER, groups=None):
    groups = groups or [[0, 1], [2, 3], [4, 5], [6, 7]]
    nc = bass.Bass("TRN2", target_bir_lowering=False)
    di = lambda n, s, t: nc.dram_tensor(n, s, t, kind="ExternalInput").ap()
    do = lambda n, s, t: nc.dram_tensor(n, s, t, kind="ExternalOutput").ap()
    xT = di("xT", [D, T], F32)
    outT = do("outT", [D, T], F32)
    cc = di("cc", [128, NCC], F32)
    pos = di("pos", [128, T], I32)
    pps = [di("pp%d" % l, [128, NPP], F32) for l in range(nlayer)]
    W = [{n: di("%s_%d" % (n, l), WSHAPES[n], F32) for n in WNAMES} for l in range(nlayer)]
    with ExitStack() as es:
        P = Prog(nc, es, pps[0], cc, pos)
        kb = P.kb
        dr = kb.dram
        S = dict(actT=dr("actT", [DFF, T], BF16), x1T=dr("x1T", [D, T], F32), zT=dr("zT", [512, T], F32),
                 qCT=dr("qCT", [512, T], BF16), qDT=dr("qDT", [512, T], BF16), gT=dr("gT", [8192, T], BF16),
                 yT=dr("yT", [2048, T], BF16), x2T=dr("x2T", [D, T], F32))
        xs = [dr("xs0", [D, T], F32), dr("xs1", [D, T], F32)]
        EX = [dr("EX%d" % i, [T, 512], BF16) for i in range(4)]
        EXZ = dr("EXZ", [512, 32], F32)
        GA = [dr("GA%d" % i, [2 * T, 512], BF16) for i in range(4)]
        GAZ = dr("GAZ", [1024, 32], F32)
        fm = lambda ap: ap.rearrange("(f a) c -> f (a c)", a=4)
        outs = []
        for l in range(nlayer):
            if l > 0:
                kb.barrier()
                kb.new_epoch()
                kb.dma("sp", P.pp[:, :], pps[l], writes=[P.pp_b])
            x_in = xT if l == 0 else xs[(l - 1) % 2]
            x_out = outT if l == nlayer - 1 else xs[l % 2]
            dA = dict(xT=x_in, w13a=W[l]["w13a"], w2a=W[l]["w2a"], w_in=W[l]["w_in"], x1T=S["x1T"], zT=S["zT"],
                      kCT=fm(EX[0]), kDT=fm(EX[1]), vC=EX[2], vD=EX[3],
                      halo=EXZ, actT=S["actT"])
            ga_keys = []

            def issue_cc(l=l, ga_keys=ga_keys):
                for sk, v in kb.ring_val.items():
                    if v > 0 and sk[1] == "sp":
                        kb._need("pool", (sk, v))
                for j, (src, dst) in enumerate([(EX[i], GA[i]) for i in range(4)] + [(EXZ, GAZ)]):
                    key = "cc_%d_%d" % (l, j)
                    kb.sems[key] = es.enter_context(nc.semaphore(key))
                    ins = nc.gpsimd.collective_compute("AllGather", ALU.bypass, replica_groups=groups,
                                                       ins=[src.opt()], outs=[dst.opt()])
                    ins.then_inc(kb.sems[key])
                    kb.n_inst += 1
                    ga_keys.append(key)

            def wait_cc(ga_keys=ga_keys):
                for e in ("pe", "act", "dve", "pool", "sp"):
                    for key in ga_keys:
                        kb._need(e, (key, 1))
            dB = dict(x1T=S["x1T"], w13b=W[l]["w13b"], w2b=W[l]["w2b"], w_in=W[l]["w_in"], w_br=W[l]["w_br"], w_out=W[l]["w_out"],
                      zT=S["zT"], kCT=dA["kCT"], kDT=dA["kDT"], vC=dA["vC"], vD=dA["vD"],
                      phalo=GAZ[0:512], pkCT=fm(GA[0][0:T]), pkDT=fm(GA[1][0:T]), pvC=GA[2][0:T], pvD=GA[3][0:T],
                      x3T=x_out, actT=S["actT"], qCT=S["qCT"], qDT=S["qDT"], gT=S["gT"], yT=S["yT"], x2T=S["x2T"])
            dB["conv_in_proj"] = True
            dA["projB"] = launchB_body(P, dict(dB, only_projB=True), after_phase1=wait_cc)
            dA["after_exports"] = issue_cc
            launchA_body(P, dA)
            dB["p1_done"] = True
            dB["conv_done"] = True
            outs = launchB_body(P, dB, after_phase1=wait_cc)
        kb.finish(outs)
        print("fused n_inst", kb.n_inst)
    return nc


def kernel(**inp):
    NCORE = 8
    x = np.asarray(inp["x"], np.float32)
    positions = np.asarray(inp["positions"], np.int32)
    nc = build_fused()
    common = {}
    for l in range(NLAYER):
        common["pp%d" % l] = make_pp(inp, l)
        common["w13a_%d" % l] = np.asarray(inp["ffn1_w13"][l], np.float32)
        common["w2a_%d" % l] = np.asarray(inp["ffn1_w2"][l], np.float32)
        common["w_in_%d" % l] = np.asarray(inp["w_in"][l], np.float32)
        common["w_br_%d" % l] = np.ascontiguousarray(np.asarray(inp["w_branch"][l], np.float32).reshape(D, D))
        common["w_out_%d" % l] = np.asarray(inp["w_out"][l], np.float32)
        common["w13b_%d" % l] = np.asarray(inp["ffn2_w13"][l], np.float32)
        common["w2b_%d" % l] = np.asarray(inp["ffn2_w2"][l], np.float32)
    in_maps = []
    for c in range(NCORE):
        b, half = c // 2, c % 2
        sl = slice(half * T, (half + 1) * T)
        m = dict(common)
        m["cc"] = make_consts(half == 1)
        m["pos"] = np.ascontiguousarray(np.broadcast_to(positions[b, sl][None, :], (128, T))).astype(np.int32)
        m["xT"] = np.ascontiguousarray(x[b, sl].T)
        in_maps.append(m)
    res = run_bass_kernel_spmd(nc, in_maps, core_ids=list(range(NCORE))).results
    out = np.zeros((4, 2 * T, D), np.float32)
    for c in range(NCORE):
        b, half = c // 2, c % 2
        out[b, half * T:(half + 1) * T] = np.asarray(res[c]["outT"], np.float32).T
    return out


def make_consts(has_prev):
    cc = np.zeros((128, NCC), np.float32)
    p = np.arange(128)
    invc = np.where(p < 32, THETA ** (-(p % 16) * 2.0 / 32), 0.0)
    pd = p % 64
    invd = np.where(pd < 16, THETA ** (-(pd % 8) * 2.0 / 16), 0.0)
    cc[:, CC_INVF] = invc; cc[:, CC_INVF + 1] = invd
    cc[:, CC_FLAG] = 1.0 if has_prev else 0.0
    cc[:, CC_FLAG + 1] = 0.0 if has_prev else NEG
    Rc = np.zeros((128, 128), np.float32)
    for m in range(16):
        Rc[m + 16, m] = -1.0; Rc[m, m + 16] = 1.0
    Rd = np.zeros((128, 128), np.float32)
    for b in (0, 64):
        for m in range(8):
            Rd[b + m + 8, b + m] = -1.0; Rd[b + m, b + m + 8] = 1.0
    cc[:, CC_RC:CC_RC + 128] = Rc; cc[:, CC_RD:CC_RD + 128] = Rd
    cc[:, CC_O128:CC_O128 + 128] = 1.0 / 128
    o64 = np.zeros((128, 128), np.float32); o64[:64, :64] = 1.0 / 64; o64[64:, 64:] = 1.0 / 64
    cc[:, CC_O64:CC_O64 + 128] = o64
    k = np.arange(128)[:, None]; q = np.arange(128)[None, :]
    cc[:, CC_M0:CC_M0 + 128] = (k <= q); cc[:, CC_M1:CC_M1 + 128] = (q <= k)
    for jr in range(4):
        for jb in range(4):
            blk = np.zeros((128, 128), np.float32) if jb < jr else ((k <= q).astype(np.float32) if jb == jr else np.ones((128, 128), np.float32))
            cc[:, CC_DM + jr * 512 + jb * 128: CC_DM + jr * 512 + (jb + 1) * 128] = blk
    return cc

def make_pp(inp, l):
    pp = np.zeros((128, NPP), np.float32)
    f = lambda k: np.asarray(inp[k][l], np.float32)
    pp[:, PP_G1:PP_G1 + 16] = f('ffn1_norm').reshape(16, 128).T
    pp[:, PP_GM:PP_GM + 16] = f('mix_norm').reshape(16, 128).T
    pp[:, PP_G2:PP_G2 + 16] = f('ffn2_norm').reshape(16, 128).T
    pp[:, PP_QKG + 0] = f('dil_q_norm'); pp[:, PP_QKG + 1] = f('dil_k_norm')
    pp[:, PP_QKG + 2] = np.tile(f('diff_q_norm'), 2); pp[:, PP_QKG + 3] = np.tile(f('diff_k_norm'), 2)
    cw = f('conv_w')
    pp[:, PP_CW:PP_CW + 124] = cw.T.reshape(4, 128, 31).transpose(1, 0, 2).reshape(128, 124)
    pp[:, PP_CB:PP_CB + 4] = f('conv_b').reshape(4, 128).T
    pp[:, PP_CG:PP_CG + 4] = f('conv_ln_g').reshape(4, 128).T
    pp[:, PP_CBE:PP_CBE + 4] = f('conv_ln_b').reshape(4, 128).T
    pp[:, PP_SUB] = f('diff_subln')
    li = 0.8 - 0.6 * math.exp(-0.3 * l)
    pp[:, PP_LI] = li; pp[:, PP_OML] = 1.0 - li
    pp[:, PP_LAM:PP_LAM + 256] = np.concatenate([f('diff_lq1'), f('diff_lk1'), f('diff_lq2'), f('diff_lk2')])[None, :]
    pp[:, PP_SG:PP_SG + 512] = f('sgu_ln_g')[None, :]
    pp[:, PP_SB:PP_SB + 512] = f('sgu_ln_b')[None, :]
    sw = f('sgu_w')
    pp[:, PP_SW:PP_SW + 512] = sw.transpose(2, 0, 1).reshape(128, 512)
    pp[:, PP_SBS:PP_SBS + 512] = f('sgu_b').reshape(1, 512)
    return pp
```
